# Optimizing a Trainium2 kernel written in Bass

```python
import jax, jax.numpy as jnp
from jax import lax
import numpy as np

D_MODEL = 2048
BATCH = 4
SEQ = 2048
DEPTH = 1

MIX_WIDTH = D_MODEL
RET_HEADS = 4
RET_QK_DIM = D_MODEL // 8
RET_V_DIM = D_MODEL // 8
RET_QK_WIDTH = RET_HEADS * RET_QK_DIM
RET_WIDTH = RET_HEADS * RET_V_DIM
ROPE_BASE = 10000.0
SSD_INNER = MIX_WIDTH - RET_WIDTH
SSD_HEAD_DIM = 64
SSD_HEADS = SSD_INNER // SSD_HEAD_DIM
SSD_GROUPS = 2
SSD_STATE = 128
SSD_CONV = 4
SSD_CONV_DIM = SSD_INNER + 2 * SSD_GROUPS * SSD_STATE
CHUNK = 128
IN_WIDTH = 2 * RET_QK_WIDTH + 2 * RET_WIDTH + SSD_INNER + SSD_CONV_DIM + SSD_HEADS
D_FF = 4 * D_MODEL
EPS = 1e-6

kernel_name = "hymba_retention_ssd_hybrid_layer"


def rmsnorm(x, w):
    xf = x.astype(jnp.float32)
    y = xf * lax.rsqrt(jnp.mean(xf * xf, axis=-1, keepdims=True) + EPS)
    return (y * w.astype(jnp.float32)).astype(x.dtype)


def rope(x, pos):
    half = x.shape[-1] // 2
    inv_freq = ROPE_BASE ** (-jnp.arange(half, dtype=jnp.float32) / half)
    ang = pos[:, None] * inv_freq[None, :]
    cos = jnp.cos(ang)[None, :, None, :]
    sin = jnp.sin(ang)[None, :, None, :]
    x1, x2 = x[..., :half], x[..., half:]
    return jnp.concatenate([x1 * cos - x2 * sin, x1 * sin + x2 * cos], axis=-1)


def retention_chunkwise(q, k, v):
    b, s, h, dk = q.shape
    dv = v.shape[-1]
    n = s // CHUNK
    log_gamma = jnp.log(1.0 - 2.0 ** (-5.0 - jnp.arange(h, dtype=jnp.float32)))
    idx = jnp.arange(CHUNK, dtype=jnp.float32)
    rel = idx[:, None] - idx[None, :]
    causal = rel >= 0
    decay_intra = jnp.where(causal[None],
                            jnp.exp(jnp.where(causal, rel, 0.0)[None] * log_gamma[:, None, None]),
                            0.0)
    k = k * (dk ** -0.5)
    qc = q.reshape(b, n, CHUNK, h, dk)
    kc = k.reshape(b, n, CHUNK, h, dk)
    vc = v.reshape(b, n, CHUNK, h, dv)
    scores = jnp.einsum('bnchd,bnmhd->bnhcm', qc, kc) * decay_intra
    y_intra = jnp.einsum('bnhcm,bnmhe->bnche', scores, vc)
    k_decay = jnp.exp((CHUNK - 1.0 - idx)[None, :] * log_gamma[:, None])
    kv = jnp.einsum('bnmhd,bnmhe,hm->nbhde', kc, vc, k_decay)
    chunk_decay = jnp.exp(CHUNK * log_gamma)[None, :, None, None]

    def step(state, kv_n):
        return chunk_decay * state + kv_n, state

    _, prev = lax.scan(step, jnp.zeros((b, h, dk, dv), kv.dtype), kv)
    q_decay = jnp.exp((idx + 1.0)[None, :] * log_gamma[:, None])
    y_cross = jnp.einsum('bnchd,nbhde,hc->bnche', qc, prev, q_decay)
    return (y_intra + y_cross).reshape(b, s, h, dv)


def ssd_chunked(x, dt, a, bmat, cmat):
    b, s, H, p = x.shape
    G, N = bmat.shape[2], bmat.shape[3]
    E = H // G
    n = s // CHUNK
    xc = (x * dt[..., None]).reshape(b, n, CHUNK, G, E, p)
    ac = (dt * a).reshape(b, n, CHUNK, G, E)
    bc = bmat.reshape(b, n, CHUNK, G, N)
    cc = cmat.reshape(b, n, CHUNK, G, N)
    a_cs = jnp.cumsum(ac, axis=2)
    seg = a_cs[:, :, :, None] - a_cs[:, :, None, :]
    causal = (jnp.arange(CHUNK)[:, None] >= jnp.arange(CHUNK)[None, :])[:, :, None, None]
    L = jnp.exp(jnp.where(causal, seg, -jnp.inf))
    cb = jnp.einsum('bnlgk,bnsgk->bnlsg', cc, bc)
    y_diag = jnp.einsum('bnlsg,bnlsge,bnsgep->bnlgep', cb, L, xc)
    decay_states = jnp.exp(a_cs[:, :, -1:] - a_cs)
    states = jnp.einsum('bnlgk,bnlge,bnlgep->nbgepk', bc, decay_states, xc)
    chunk_decay = jnp.moveaxis(jnp.exp(a_cs[:, :, -1]), 1, 0)

    def step(state, inp):
        st, dec = inp
        return dec[..., None, None] * state + st, state

    _, prev = lax.scan(step, jnp.zeros((b, G, E, p, N), states.dtype), (states, chunk_decay))
    y_off = jnp.einsum('bnlgk,nbgepk,bnlge->bnlgep', cc, prev, jnp.exp(a_cs))
    return (y_diag + y_off).reshape(b, s, H, p)


def causal_depthwise_conv(u, w, bias):
    K, c = w.shape
    out = lax.conv_general_dilated(u, w[:, None, :], window_strides=(1,), padding=[(K - 1, 0)],
                                   dimension_numbers=('NWC', 'WIO', 'NWC'), feature_group_count=c)
    return out + bias


def hybrid_mixer(h, w_in, ret_norm_w, conv_w, conv_b, dt_bias, a_log, d_skip, ssd_norm_w, w_out):
    b, s, _ = h.shape
    f32 = jnp.float32
    proj = jnp.einsum('bsd,df->bsf', h, w_in)
    o1 = RET_QK_WIDTH
    o2 = o1 + RET_QK_WIDTH
    o3 = o2 + RET_WIDTH
    o4 = o3 + RET_WIDTH
    o5 = o4 + SSD_INNER
    o6 = o5 + SSD_CONV_DIM
    q, k, v, g, z, xbc, dt = jnp.split(proj, [o1, o2, o3, o4, o5, o6], axis=-1)

    pos = jnp.arange(s, dtype=f32)
    q = rope(q.astype(f32).reshape(b, s, RET_HEADS, RET_QK_DIM), pos)
    k = rope(k.astype(f32).reshape(b, s, RET_HEADS, RET_QK_DIM), pos)
    v = v.astype(f32).reshape(b, s, RET_HEADS, RET_V_DIM)
    y_ret = retention_chunkwise(q, k, v)
    y_ret = y_ret * lax.rsqrt(jnp.mean(y_ret * y_ret, axis=-1, keepdims=True) + EPS)
    y_ret = y_ret * ret_norm_w.astype(f32).reshape(RET_HEADS, RET_V_DIM)
    y_ret = y_ret.reshape(b, s, RET_WIDTH) * jax.nn.silu(g.astype(f32))

    xbc = jax.nn.silu(causal_depthwise_conv(xbc.astype(f32), conv_w.astype(f32), conv_b.astype(f32)))
    xs, bm, cm = jnp.split(xbc, [SSD_INNER, SSD_INNER + SSD_GROUPS * SSD_STATE], axis=-1)
    xs = xs.reshape(b, s, SSD_HEADS, SSD_HEAD_DIM)
    bm = bm.reshape(b, s, SSD_GROUPS, SSD_STATE)
    cm = cm.reshape(b, s, SSD_GROUPS, SSD_STATE)
    dt = jax.nn.softplus(dt.astype(f32) + dt_bias.astype(f32))
    a = -jnp.exp(a_log.astype(f32))
    y_ssd = ssd_chunked(xs, dt, a, bm, cm) + d_skip.astype(f32)[:, None] * xs
    y_ssd = y_ssd.reshape(b, s, SSD_INNER) * jax.nn.silu(z.astype(f32))
    y_ssd = y_ssd.reshape(b, s, SSD_GROUPS, SSD_INNER // SSD_GROUPS)
    y_ssd = y_ssd * lax.rsqrt(jnp.mean(y_ssd * y_ssd, axis=-1, keepdims=True) + EPS)
    y_ssd = y_ssd.reshape(b, s, SSD_INNER) * ssd_norm_w.astype(f32)

    mix = jnp.concatenate([y_ret, y_ssd], axis=-1).astype(h.dtype)
    return jnp.einsum('bsf,fd->bsd', mix, w_out)


def squared_relu_mlp(h, w_up, w_down):
    u = jax.nn.relu(jnp.einsum('bsd,df->bsf', h, w_up))
    return jnp.einsum('bsf,fd->bsd', u * u, w_down)


def setup_inputs(seed: int = 0) -> dict:
    key = jax.random.key(seed)
    ks = jax.random.split(key, 16)
    f32 = jnp.float32
    x = jax.random.normal(ks[0], (BATCH, SEQ, D_MODEL), f32)
    norm_mix_w = 1.0 + 0.01 * jax.random.normal(ks[1], (D_MODEL,), f32)
    w_in = jax.random.normal(ks[2], (D_MODEL, IN_WIDTH), f32) * D_MODEL ** -0.5
    ret_norm_w = 1.0 + 0.01 * jax.random.normal(ks[3], (RET_WIDTH,), f32)
    conv_w = jax.random.normal(ks[4], (SSD_CONV, SSD_CONV_DIM), f32) * SSD_CONV ** -0.5
    conv_b = 0.01 * jax.random.normal(ks[5], (SSD_CONV_DIM,), f32)
    dt0 = jnp.exp(jax.random.uniform(ks[6], (SSD_HEADS,), f32, jnp.log(1e-3), jnp.log(1e-1)))
    dt_bias = dt0 + jnp.log(-jnp.expm1(-dt0))
    a_log = jnp.log(jax.random.uniform(ks[7], (SSD_HEADS,), f32, 1.0, 16.0))
    d_skip = 1.0 + 0.01 * jax.random.normal(ks[8], (SSD_HEADS,), f32)
    ssd_norm_w = 1.0 + 0.01 * jax.random.normal(ks[9], (SSD_INNER,), f32)
    w_out = jax.random.normal(ks[10], (MIX_WIDTH, D_MODEL), f32) * MIX_WIDTH ** -0.5
    norm_mlp_w = 1.0 + 0.01 * jax.random.normal(ks[11], (D_MODEL,), f32)
    w_up = jax.random.normal(ks[12], (D_MODEL, D_FF), f32) * D_MODEL ** -0.5
    w_down = jax.random.normal(ks[13], (D_FF, D_MODEL), f32) * D_FF ** -0.5
    norm_final_w = 1.0 + 0.01 * jax.random.normal(ks[14], (D_MODEL,), f32)
    return {"x": x, "norm_mix_w": norm_mix_w, "w_in": w_in, "ret_norm_w": ret_norm_w,
            "conv_w": conv_w, "conv_b": conv_b, "dt_bias": dt_bias, "a_log": a_log,
            "d_skip": d_skip, "ssd_norm_w": ssd_norm_w, "w_out": w_out,
            "norm_mlp_w": norm_mlp_w, "w_up": w_up, "w_down": w_down,
            "norm_final_w": norm_final_w}


def reference(x, norm_mix_w, w_in, ret_norm_w, conv_w, conv_b, dt_bias, a_log, d_skip,
              ssd_norm_w, w_out, norm_mlp_w, w_up, w_down, norm_final_w):
    h = x
    for _ in range(DEPTH):
        h = h + hybrid_mixer(rmsnorm(h, norm_mix_w), w_in, ret_norm_w, conv_w, conv_b,
                             dt_bias, a_log, d_skip, ssd_norm_w, w_out)
        h = h + squared_relu_mlp(rmsnorm(h, norm_mlp_w), w_up, w_down)
    return rmsnorm(h, norm_final_w)
```

```python
import contextlib
import numpy as np
import concourse.bass as bass
import concourse.mybir as mybir
from concourse.bass_utils import run_bass_kernel_spmd

F32 = mybir.dt.float32
BF16 = mybir.dt.bfloat16
U8 = mybir.dt.uint8
AF = mybir.ActivationFunctionType
ALU = mybir.AluOpType

NDSEM = 8
EPS = 1e-6
D = 2048
NT = 1024
O_Q, O_K, O_V, O_G, O_Z, O_X, O_B, O_C, O_DT = 0, 1024, 2048, 3072, 4096, 5120, 6144, 6400, 6656
IN_W = 6672
NSLAB = 5
C_DMASK, C_TRI, C_NEG, C_ID, C_KD, C_QD, C_FLAG, C_END = 0, 512, 640, 1152, 1280, 1284, 1288, 1289


class _St:
    __slots__ = ("w", "r")

    def __init__(self):
        self.w = None
        self.r = []


class Prog:
    ENG = ("pe", "act", "dve", "pool", "sp")

    def __init__(self, nc):
        self.nc = nc
        self.ops = {e: [] for e in self.ENG}
        self.res = {}
        self.ndma = {e: 0 for e in self.ENG}

    def _get(self, name):
        if name not in self.res:
            self.res[name] = {"whole": _St(), "parts": {}}
        return self.res[name]

    @staticmethod
    def _norm(r):
        if isinstance(r, tuple):
            return r[0], r[1]
        return r, None

    def op(self, eng, fn, reads=(), writes=(), dma=False):
        deps = set()
        me = ("dma", eng, self.ndma[eng]) if dma else ("op", eng, len(self.ops[eng]))
        for r in reads:
            name, idx = self._norm(r)
            e = self._get(name)
            if e["whole"].w:
                deps.add(e["whole"].w)
            if idx is None:
                for p in e["parts"].values():
                    if p.w:
                        deps.add(p.w)
            else:
                p = e["parts"].get(idx)
                if p is not None and p.w:
                    deps.add(p.w)
        for r in writes:
            name, idx = self._norm(r)
            e = self._get(name)
            wh = e["whole"]
            if wh.w:
                deps.add(wh.w)
            deps.update(wh.r)
            if idx is None:
                for p in e["parts"].values():
                    if p.w:
                        deps.add(p.w)
                    deps.update(p.r)
            else:
                p = e["parts"].get(idx)
                if p is not None:
                    if p.w:
                        deps.add(p.w)
                    deps.update(p.r)
        for r in reads:
            name, idx = self._norm(r)
            e = self._get(name)
            if idx is None:
                e["whole"].r.append(me)
            else:
                e["parts"].setdefault(idx, _St()).r.append(me)
        for r in writes:
            name, idx = self._norm(r)
            e = self._get(name)
            st = _St()
            st.w = me
            if idx is None:
                e["whole"] = st
                e["parts"] = {}
            else:
                e["parts"][idx] = st
        deps.discard(me)
        if eng == "pe":
            deps = {d for d in deps if not (d[0] == "op" and d[1] == "pe")}
        self._add(eng, fn, deps, dma)
        return me

    def _add(self, eng, fn, deps, dma):
        rec = {"fn": fn, "deps": deps, "dma": dma, "signal": False,
               "dma_idx": self.ndma[eng] if dma else None}
        if dma:
            self.ndma[eng] += 1
        for d in deps:
            if d[0] == "op":
                self.ops[d[1]][d[2]]["signal"] = True
        self.ops[eng].append(rec)

    def barrier(self):
        deps = set()
        for e in self.ENG:
            for i in range(len(self.ops[e]) - 1, -1, -1):
                if not self.ops[e][i]["dma"] and self.ops[e][i]["fn"] is not None:
                    deps.add(("op", e, i))
                    break
            n = self.ndma[e]
            for i in range(max(0, n - NDSEM), n):
                deps.add(("dma", e, i))
        for e in self.ENG:
            d = {x for x in deps if not (x[0] == "op" and x[1] == e and e == "pe")}
            self._add(e, None, d, False)
        self.res = {}

    def emit(self):
        nc = self.nc
        with contextlib.ExitStack() as es:
            sems = {e: es.enter_context(nc.semaphore("s_" + e)) for e in self.ENG if e != "sp"}
            dsems = {}
            for q in ("sp", "act", "pool"):
                if self.ndma[q]:
                    dsems[q] = [es.enter_context(nc.semaphore(f"d_{q}{i}")) for i in range(NDSEM)]
            for e in self.ENG:
                c = 0
                for rec in self.ops[e]:
                    if rec["signal"] and not rec["dma"]:
                        c += 1
                        rec["sig"] = c
            block = es.enter_context(nc.Block())
            ops = self.ops

            def run(e, eng):
                waited = {}

                def wait(key, sem, val):
                    if waited.get(key, 0) < val:
                        eng.wait_ge(sem, val)
                        waited[key] = val

                for rec in ops[e]:
                    for d in sorted(rec["deps"]):
                        if d[0] == "op":
                            wait(("op", d[1]), sems[d[1]], ops[d[1]][d[2]]["sig"])
                        else:
                            _, q, i = d
                            wait(("dma", q, i % NDSEM), dsems[q][i % NDSEM], 16 * (i // NDSEM + 1))
                    if rec["dma"]:
                        i = rec["dma_idx"]
                        if i >= NDSEM:
                            wait(("dma", e, i % NDSEM), dsems[e][i % NDSEM], 16 * (i // NDSEM))
                        rec["fn"](eng).then_inc(dsems[e][i % NDSEM], 16)
                    elif rec["fn"] is not None:
                        ins = rec["fn"](eng)
                        if rec["signal"]:
                            ins.then_inc(sems[e], 1)

            @block.tensor
            def _(eng):
                run("pe", eng)

            @block.scalar
            def _(eng):
                run("act", eng)

            @block.vector
            def _(eng):
                run("dve", eng)

            @block.gpsimd
            def _(eng):
                run("pool", eng)

            @block.sync
            def _(eng):
                run("sp", eng)


def build_program(dbg=False):
    nc = bass.Bass("TRN2", target_bir_lowering=False)

    def din(name, shape):
        return nc.dram_tensor(name, shape, F32, kind="ExternalInput").ap()

    xm = din("xm", [NT, D])
    xp = din("xp", [NT, D])
    w_in = din("w_in", [D, IN_W])
    w_out = din("w_out", [D, D])
    w_up = din("w_up", [D, 4 * D])
    w_down = din("w_down", [4 * D, D])
    norm_mix_w = din("norm_mix_w", [D])
    ret_norm_w = din("ret_norm_w", [1024])
    conv_w = din("conv_w", [4, 1536])
    conv_b = din("conv_b", [1536])
    dt_bias = din("dt_bias", [16])
    a_log = din("a_log", [16])
    d_skip = din("d_skip", [16])
    ssd_norm_w = din("ssd_norm_w", [1024])
    norm_mlp_w = din("norm_mlp_w", [D])
    norm_final_w = din("norm_final_w", [D])
    cst = din("cst", [128, C_END])
    rope = din("rope", [128, 2, 2, NT])
    yout = nc.dram_tensor("y", [NT, D], F32, kind="ExternalOutput").ap()
    if dbg:
        dbg_mix = nc.dram_tensor("dbg_mix", [NT, D], BF16, kind="ExternalOutput").ap()
        dbg_h1 = nc.dram_tensor("dbg_h1", [NT, D], F32, kind="ExternalOutput").ap()

    P = Prog(nc)
    es = contextlib.ExitStack()
    arena = es.enter_context(nc.sbuf_tensor("arena", [128, 206 * 1024], U8))
    off = [0]

    def alloc(shape, dt):
        sz = 4 if dt == F32 else 2
        n = int(np.prod(shape)) * sz
        v = arena[:, off[0]:off[0] + n].bitcast(dt)
        off[0] += (n + 63) // 64 * 64
        if len(shape) == 2:
            v = v.rearrange("p (a b) -> p a b", a=shape[0])
        elif len(shape) == 3:
            v = v.rearrange("p (a b c) -> p a b c", a=shape[0], b=shape[1])
        return v

    banks = [es.enter_context(nc.psum_tensor(f"ps{i}", [128, 512], F32)) for i in range(8)]
    psc = [0]

    def ps():
        i = psc[0] % 8
        psc[0] += 1
        return banks[i][:], f"ps{i}"

    CST = alloc([C_END], F32)
    slabs = [alloc([4096], BF16) for _ in range(NSLAB)]
    Wdt = alloc([16, 16], BF16)
    s_off = off[0]
    S = alloc([4, 512], F32)
    Sbf = alloc([512], BF16)
    Sst = alloc([2, 512], F32)
    Sstbf = alloc([2, 512], BF16)
    hist = alloc([12, 3], F32)
    SM = alloc([8, 8, 16], F32)
    K_DT, K_AC, K_ACS, K_AL, K_DTDS, K_EA, K_CD, K_NACS = range(8)
    rnw = alloc([1024], F32)
    snw = alloc([1024], F32)
    cw = alloc([4, 12], F32)
    cb = alloc([12], F32)
    nw = alloc([16], F32)
    nmw = alloc([16], F32)
    dtb = alloc([16], F32)
    abc = alloc([16], F32)
    dsk = alloc([16], F32)
    ones = alloc([128], F32)
    identb = alloc([128], BF16)
    st1 = alloc([16], F32)
    regA = alloc([16, 1024], BF16)
    regB = alloc([8, 2048], BF16)
    c_off = off[0]
    CBYTES = 64 * 1024
    assert c_off + CBYTES <= 206 * 1024, c_off

    def carve():
        off[0] = c_off

    dmask = CST[:, C_DMASK:C_DMASK + 512].rearrange("p (h c) -> p h c", h=4)
    tri = CST[:, C_TRI:C_TRI + 128]
    negm = CST[:, C_NEG:C_NEG + 512]
    ident = CST[:, C_ID:C_ID + 128]
    kd = CST[:, C_KD:C_KD + 4]
    qd = CST[:, C_QD:C_QD + 4]
    flag = CST[:, C_FLAG:C_FLAG + 1]
    gam = [1.0 - 2.0 ** (-5.0 - h) for h in range(4)]
    cdec = [float(np.float32(np.exp(np.float32(128.0) * np.log(np.float32(g))))) for g in gam]

    def dma(q, out, in_, reads, writes, slow=False):
        P.op(q, lambda e: e.dma_start(out=out, in_=in_, allow_slow_non_contiguous=slow), reads, writes, dma=True)

    def mm(out, lhsT, rhs, start, stop, reads, writes):
        P.op("pe", lambda e: e.matmul(out, lhsT=lhsT, rhs=rhs, start=start, stop=stop), reads, writes)

    def tr(out, in_, idn, reads, writes):
        P.op("pe", lambda e: e.transpose(out=out, in_=in_, identity=idn), reads, writes)

    def act(out, in_, func, reads, writes, **kw):
        P.op("act", lambda e: e.activation(out=out, in_=in_, func=func, **kw), reads, writes)

    def tt(out, in0, in1, op, reads, writes, eng="dve"):
        P.op(eng, lambda e: e.tensor_tensor(out=out, in0=in0, in1=in1, op=op), reads, writes)

    def tsc(out, in0, s1, s2, op0, op1, reads, writes, eng="dve"):
        P.op(eng, lambda e: e.tensor_scalar(out=out, in0=in0, scalar1=s1, scalar2=s2, op0=op0, op1=op1), reads, writes)

    def stt(out, in0, scalar, in1, op0, op1, reads, writes, eng="dve"):
        P.op(eng, lambda e: e.scalar_tensor_tensor(out=out, in0=in0, scalar=scalar, in1=in1, op0=op0, op1=op1), reads, writes)

    def cp(out, in_, reads, writes, eng="dve"):
        P.op(eng, lambda e: e.tensor_copy(out=out, in_=in_), reads, writes)

    slabc = [0]

    def slab_load(parts, shape):
        i = slabc[0] % NSLAB
        slabc[0] += 1
        kc, cols = shape
        v = slabs[i].rearrange("p (k c) -> p k c", k=kc)
        for (c0, c1, src) in parts:
            dma("pool", v[:, :, c0:c1], src.rearrange("(k p) c -> p k c", p=128), [], [f"slab{i}"])
        return v, f"slab{i}"

    def w_slab(w, r0, rows, c0, cols):
        return slab_load([(0, cols, w[r0:r0 + rows, c0:c0 + cols])], (rows // 128, cols))

    def rstd_from_ss(ss_ap, n, res):
        act(ss_ap, ss_ap, AF.Ln, [res, "st1"], [res], scale=1.0 / n, bias=epsb[:, 0:1])
        act(ss_ap, ss_ap, AF.Exp, [res], [res], scale=-0.5)

    epsb = alloc([1], F32) if False else st1[:, 15:16]
    dma("sp", CST, cst, [], ["CST"])
    P.op("dve", lambda e: e.memset(st1, EPS), [], ["st1"])
    P.op("dve", lambda e: e.memset(ones, 1.0), [], ["ones"])
    P.op("dve", lambda e: e.memset(S, 0.0), [], ["S"])
    P.op("dve", lambda e: e.memset(Sst, 0.0), [], ["Sst"])
    P.op("dve", lambda e: e.memset(Sstbf, 0.0), [], ["Sstbf"])
    P.op("dve", lambda e: e.memset(hist, 0.0), [], ["hist"])
    cp(identb, ident, ["CST"], ["identb"])
    dma("sp", rnw, ret_norm_w.partition_broadcast(128), [], ["rnw"])
    dma("sp", snw, ssd_norm_w.partition_broadcast(128), [], ["snw"])
    dma("sp", dtb, dt_bias.partition_broadcast(128), [], ["dtb"])
    dma("sp", abc, a_log.partition_broadcast(128), [], ["abc"])
    dma("sp", dsk, d_skip.partition_broadcast(128), [], ["dsk"])
    dma("sp", nw, norm_mix_w.rearrange("(c p) -> p c", p=128), [], ["nw"], slow=True)
    dma("sp", nmw, norm_mlp_w.rearrange("(c p) -> p c", p=128), [], ["nmw"], slow=True)
    dma("sp", cb, conv_b.rearrange("(c p) -> p c", p=128), [], ["cb"], slow=True)
    for jj in range(4):
        dma("sp", cw[:, jj, :], conv_w[jj, :].rearrange("(c p) -> p c", p=128), [], ["cw"], slow=True)
    act(abc, abc, AF.Exp, ["abc"], ["abc"])
    tsc(abc, abc, -1.0, None, ALU.mult, ALU.bypass, ["abc"], ["abc"]) if False else \
        P.op("dve", lambda e: e.tensor_scalar_mul(out=abc, in0=abc, scalar1=-1.0), ["abc"], ["abc"])
    dma("pool", Wdt, w_in[:, O_DT:O_DT + 16].rearrange("(k p) c -> p k c", p=128), [], ["Wdt"])

    xnT = regA
    mix = regB

    def norm_transpose(load_fn, src_res, wvec, wres, dstT, dst_res, tag):
        carve_base = off[0]
        junk = alloc([2048], BF16)
        xb = [alloc([2048], BF16) for _ in range(2)]
        ssq = alloc([8], F32)
        for c in range(8):
            xt, xres = load_fn(c)
            act(junk, xt, AF.Square, [xres], [tag + "junk", (tag + "ss", c)], accum_out=ssq[:, c:c + 1])
            rstd_from_ss(ssq[:, c:c + 1], 2048, (tag + "ss", c))
            xbc = xb[c % 2]
            tsc(xbc, xt, ssq[:, c:c + 1], None, ALU.mult, ALU.bypass, [xres, (tag + "ss", c)], [(tag + "xb", c % 2)]) if False else \
                P.op("dve", lambda e, xbc=xbc, xt=xt, c=c: e.tensor_scalar_mul(out=xbc, in0=xt, scalar1=ssq[:, c:c + 1]),
                     [xres, (tag + "ss", c)], [(tag + "xb", c % 2)])
            for half in range(2):
                pt, pres = ps()
                ptb = pt.bitcast(BF16)
                for k in range(8):
                    dc = half * 8 + k
                    tr(ptb[:, k * 128:(k + 1) * 128], xbc[:, dc * 128:(dc + 1) * 128], identb,
                       [(tag + "xb", c % 2), "identb"], [pres])
                tt(dstT[:, half * 8:(half + 1) * 8, c * 128:(c + 1) * 128],
                   ptb[:, 0:1024].rearrange("p (k t) -> p k t", k=8),
                   wvec[:, half * 8:(half + 1) * 8].unsqueeze(2).to_broadcast([128, 8, 128]),
                   ALU.mult, [pres, wres], [(dst_res, c)])
        off[0] = carve_base

    for pas in range(2):
        main = pas == 1
        xsrc = xm if main else xp
        carve()
        xbuf = [alloc([2048], F32) for _ in range(2)]

        def load_x(c, xsrc=xsrc, xbuf=xbuf):
            b = xbuf[c % 2]
            dma("sp", b, xsrc[c * 128:(c + 1) * 128, :], [], [("xbuf", c % 2)])
            return b, ("xbuf", c % 2)

        norm_transpose(load_x, None, nw, "nw", xnT, "xnT", "nx")
        P.barrier()

        carve()
        cs = alloc([2, NT], F32)
        dma("sp", cs, rope[:, pas, :, :], [], ["cs"])
        kTr = [alloc([512], BF16) for _ in range(2)]
        qTr = [alloc([512], BF16) for _ in range(2)]
        ktok = alloc([4, 256], BF16)
        vtok = alloc([4, 256], BF16)
        gs = alloc([4, 256], F32)
        rt = [alloc([512], F32) for _ in range(2)]
        sT = alloc([128], BF16)
        tmpY = alloc([256], F32)
        yb = alloc([256], F32)
        yb2 = alloc([256], F32)
        junkr = alloc([256], BF16)

        def proj_rope(W, wres, t0, dst, dres):
            pk = []
            for a in range(2):
                p_, r_ = ps()
                for dc in range(16):
                    mm(p_[:, 0:512], W[:, dc, a * 128:(a + 1) * 128], xnT[:, dc, t0:t0 + 512], dc == 0, dc == 15,
                       [wres, "xnT"], [r_])
                pk.append((p_, r_))
            cosv = cs[:, 0, t0:t0 + 512]
            sinv = cs[:, 1, t0:t0 + 512]
            (p0, r0), (p1, r1) = pk
            tt(rt[0], p0[:, 0:512], cosv, ALU.mult, [r0, "cs"], ["rt0"])
            tt(rt[1], p1[:, 0:512], sinv, ALU.mult, [r1, "cs"], ["rt1"])
            tt(dst[0], rt[0], rt[1], ALU.subtract, ["rt0", "rt1"], [(dres, 0)])
            tt(rt[0], p0[:, 0:512], sinv, ALU.mult, [r0, "cs"], ["rt0"])
            tt(rt[1], p1[:, 0:512], cosv, ALU.mult, [r1, "cs"], ["rt1"])
            tt(dst[1], rt[0], rt[1], ALU.add, ["rt0", "rt1"], [(dres, 1)])

        def proj_tok(W, wres, t0, func, dst, dres):
            for jj in range(2):
                p_, r_ = ps()
                for j2 in range(2):
                    j = jj * 2 + j2
                    for dc in range(16):
                        mm(p_[:, j2 * 256:(j2 + 1) * 256], xnT[:, dc, t0 + j * 128:t0 + (j + 1) * 128], W[:, dc, :],
                           dc == 0, dc == 15, [wres, "xnT"], [r_])
                act(dst[:, jj * 2:(jj + 1) * 2, :], p_[:, 0:512].rearrange("p (j c) -> p j c", j=2), func,
                    [r_], [dres])

        for h in range(4):
            Wk, wkr = w_slab(w_in, 0, D, O_K + h * 256, 256)
            Wv, wvr = w_slab(w_in, 0, D, O_V + h * 256, 256)
            if main:
                Wq, wqr = w_slab(w_in, 0, D, O_Q + h * 256, 256)
                Wg, wgr = w_slab(w_in, 0, D, O_G + h * 256, 256)
            Sh = S[:, h, :]
            cp(Sbf, Sh, ["S"], ["Sbf"], eng="act") if False else \
                act(Sbf, Sh, AF.Copy, [("S", h)], ["Sbf"])
            for blk in range(2):
                t0 = blk * 512
                proj_rope(Wk, wkr, t0, kTr, "kTr")
                pt, pres = ps()
                ptb = pt.bitcast(BF16)
                for j in range(4):
                    for a in range(2):
                        tr(ptb[:, j * 256 + a * 128:j * 256 + (a + 1) * 128], kTr[a][:, j * 128:(j + 1) * 128], identb,
                           [("kTr", a), "identb"], [pres])
                act(ktok, ptb[:, 0:1024].rearrange("p (j c) -> p j c", j=4), AF.Copy, [pres, "CST"], ["ktok"],
                    scale=kd[:, h:h + 1])
                proj_tok(Wv, wvr, t0, AF.Copy, vtok, "vtok")
                if main:
                    proj_rope(Wq, wqr, t0, qTr, "qTr")
                    proj_tok(Wg, wgr, t0, AF.Silu, gs, "gs")
                for j in range(4):
                    c = blk * 4 + j
                    tsl = slice(j * 128, (j + 1) * 128)
                    if main:
                        pS, rS = ps()
                        for a in range(2):
                            mm(pS[:, 0:128], kTr[a][:, tsl], qTr[a][:, tsl], a == 0, a == 1,
                               [("kTr", a), ("qTr", a)], [rS])
                        tt(sT, pS[:, 0:128], dmask[:, h, :], ALU.mult, [rS, "CST"], ["sT"])
                        pY, rY = ps()
                        mm(pY[:, 0:256], sT, vtok[:, j, :], True, True, ["sT", "vtok"], [rY])
                        for a in range(2):
                            mm(pY[:, 256:512], qTr[a][:, tsl], Sbf[:, a * 256:(a + 1) * 256], a == 0, a == 1,
                               [("qTr", a), "Sbf"], [rY])
                        act(tmpY, pY[:, 256:512], AF.Copy, [rY, "CST"], ["tmpY"], scale=qd[:, h:h + 1])
                        tt(yb, pY[:, 0:256], tmpY, ALU.add, [rY, "tmpY"], ["yb"])
                        act(junkr, yb, AF.Square, ["yb"], ["junkr", "ssr"], accum_out=st1[:, 0:1])
                        rstd_from_ss(st1[:, 0:1], 256, "ssr")
                        stt(yb2, yb, st1[:, 0:1], rnw[:, h * 256:(h + 1) * 256], ALU.mult, ALU.mult,
                            ["yb", "ssr", "rnw"], ["yb2"])
                        tt(mix[:, c, h * 256:(h + 1) * 256], yb2, gs[:, j, :], ALU.mult, ["yb2", "gs"], [("mix", c)])
                    if not (main and c == 7):
                        pK, rK = ps()
                        for a in range(2):
                            mm(pK[:, a * 256:(a + 1) * 256], ktok[:, j, a * 128:(a + 1) * 128], vtok[:, j, :], True, True,
                               ["ktok", "vtok"], [rK])
                        stt(Sh, Sh, cdec[h], pK[:, 0:512], ALU.mult, ALU.add, [rK, ("S", h)], [("S", h)])
                        act(Sbf, Sh, AF.Copy, [("S", h)], ["Sbf"])
        P.barrier()

        carve()
        U = alloc([6, 515], F32)
        acc = alloc([512], F32)
        xsT = alloc([4, 512], F32)
        BT = alloc([512], BF16)
        CTt = alloc([512], BF16)
        xstok = alloc([4, 512], F32)
        Btok = alloc([4, 128], BF16)
        zs = alloc([4, 512], F32)
        Xb = alloc([8, 128], F32)
        Lb = alloc([8, 128], F32)
        cbs = alloc([128], F32)
        Mb = alloc([8, 128], BF16)
        xc = alloc([512], BF16)
        xcd = alloc([512], BF16)
        t1 = alloc([512], F32)
        t2 = alloc([512], F32)
        yg = alloc([512], F32)
        junks = alloc([512], BF16)
        tsm = alloc([16], F32)
        assert off[0] <= c_off + CBYTES, off[0] - c_off

        def bh(ap16, g):
            return ap16[:, g * 8:(g + 1) * 8].unsqueeze(2).to_broadcast([128, 8, 64])

        def v3(ap512):
            return ap512.rearrange("p (h q) -> p h q", h=8)

        for g in range(2):
            Wx0, rx0 = w_slab(w_in, 0, D, O_X + g * 512, 256)
            Wx1, rx1 = w_slab(w_in, 0, D, O_X + g * 512 + 256, 256)
            Wbc, rbc = slab_load([(0, 128, w_in[:, O_B + g * 128:O_B + (g + 1) * 128]),
                                  (128, 256, w_in[:, O_C + g * 128:O_C + (g + 1) * 128])], (16, 256))
            if main:
                Wz0, rz0 = w_slab(w_in, 0, D, O_Z + g * 512, 256)
                Wz1, rz1 = w_slab(w_in, 0, D, O_Z + g * 512 + 256, 256)
            for blk in range(2):
                t0 = blk * 512
                for t in range(6):
                    W, wr = ((Wx0, rx0), (Wx0, rx0), (Wx1, rx1), (Wx1, rx1), (Wbc, rbc), (Wbc, rbc))[t]
                    co = (t % 2) * 128
                    ch = (g * 4 + t) if t < 4 else (8 + g if t == 4 else 10 + g)
                    pU, rU = ps()
                    for dc in range(16):
                        mm(pU[:, 0:512], W[:, dc, co:co + 128], xnT[:, dc, t0:t0 + 512], dc == 0, dc == 15,
                           [wr, "xnT"], [rU])
                    cp(U[:, t, 0:3], hist[:, ch, :], [("hist", ch)], [("U", t)])
                    act(U[:, t, 3:515], pU[:, 0:512], AF.Copy, [rU], [("U", t)])
                    cp(hist[:, ch, :], U[:, t, 512:515], [("U", t)], [("hist", ch)])
                    tsc(acc, U[:, t, 0:512], cw[:, 0, ch:ch + 1], cb[:, ch:ch + 1], ALU.mult, ALU.add,
                        [("U", t), "cw", "cb"], ["acc"])
                    for jj in range(1, 4):
                        stt(acc, U[:, t, jj:jj + 512], cw[:, jj, ch:ch + 1], acc, ALU.mult, ALU.add,
                            [("U", t), "cw", "acc"], ["acc"])
                    if t < 4:
                        act(xsT[:, t, :], acc, AF.Silu, ["acc"], [("xsT", t)])
                    elif t == 4:
                        act(BT, acc, AF.Silu, ["acc"], ["BT"])
                    else:
                        act(CTt, acc, AF.Silu, ["acc"], ["CT"])
                for j in range(4):
                    pX, rX = ps()
                    for t in range(4):
                        tr(pX[:, t * 128:(t + 1) * 128], xsT[:, t, j * 128:(j + 1) * 128], ident, [("xsT", t), "CST"], [rX])
                    act(xstok[:, j, :], pX[:, 0:512], AF.Copy, [rX], [("xstok", j)])
                pB, rB = ps()
                pBb = pB.bitcast(BF16)
                for j in range(4):
                    tr(pBb[:, j * 128:(j + 1) * 128], BT[:, j * 128:(j + 1) * 128], identb, ["BT", "identb"], [rB])
                cp(Btok, pBb[:, 0:512].rearrange("p (j c) -> p j c", j=4), [rB], ["Btok"])
                if main:
                    for j in range(4):
                        pZ, rZ = ps()
                        for q, (Wz, rz) in enumerate(((Wz0, rz0), (Wz1, rz1))):
                            for dc in range(16):
                                mm(pZ[:, q * 256:(q + 1) * 256], xnT[:, dc, t0 + j * 128:t0 + (j + 1) * 128], Wz[:, dc, :],
                                   dc == 0, dc == 15, [rz, "xnT"], [rZ])
                        act(zs[:, j, :], pZ[:, 0:512], AF.Silu, [rZ], [("zs", j)])
                if g == 0:
                    for j in range(4):
                        c = blk * 4 + j
                        pD, rD = ps()
                        for dc in range(16):
                            mm(pD[:, 0:16], xnT[:, dc, t0 + j * 128:t0 + (j + 1) * 128], Wdt[:, dc, :], dc == 0, dc == 15,
                               ["Wdt", "xnT"], [rD])
                        sm = ("SM", c)
                        tt(tsm, pD[:, 0:16], dtb, ALU.add, [rD, "dtb"], ["tsm"])
                        act(tsm, tsm, AF.Exp, ["tsm"], ["tsm"])
                        act(SM[:, K_DT, c, :], tsm, AF.Ln, ["tsm", "ones"], [sm], bias=ones[:, 0:1])
                        tt(SM[:, K_AC, c, :], SM[:, K_DT, c, :], abc, ALU.mult, [sm, "abc"], [sm])
                        pA, rA = ps()
                        mm(pA[:, 0:16], tri, SM[:, K_AC, c, :], True, True, ["CST", sm], [rA])
                        mm(pA[:, 16:32], ones, SM[:, K_AC, c, :], True, True, ["ones", sm], [rA])
                        act(SM[:, K_ACS, c, :], pA[:, 0:16], AF.Copy, [rA], [sm])
                        act(SM[:, K_AL, c, :], pA[:, 16:32], AF.Copy, [rA], [sm])
                        tt(tsm, SM[:, K_AL, c, :], SM[:, K_ACS, c, :], ALU.subtract, [sm], ["tsm"])
                        act(tsm, tsm, AF.Exp, ["tsm"], ["tsm"])
                        tt(SM[:, K_DTDS, c, :], tsm, SM[:, K_DT, c, :], ALU.mult, ["tsm", sm], [sm])
                        act(SM[:, K_EA, c, :], SM[:, K_ACS, c, :], AF.Exp, [sm], [sm])
                        act(SM[:, K_CD, c, :], SM[:, K_AL, c, :], AF.Exp, [sm], [sm])
                        act(SM[:, K_NACS, c, :], SM[:, K_ACS, c, :], AF.Copy, [sm], [sm], scale=-1.0)
                for j in range(4):
                    c = blk * 4 + j
                    sm = ("SM", c)
                    tsl = slice(j * 128, (j + 1) * 128)
                    xsj = v3(xstok[:, j, :])
                    if main:
                        tt(Xb, SM[:, K_AC, c, g * 8:(g + 1) * 8].unsqueeze(2).to_broadcast([128, 8, 128]),
                           tri.unsqueeze(1).to_broadcast([128, 8, 128]), ALU.mult, [sm, "CST"], ["Xb"])
                        pR = []
                        for q in range(2):
                            p_, r_ = ps()
                            mm(p_[:, 0:512], ones, Xb[:, q * 4:(q + 1) * 4, :].rearrange("p h l -> p (h l)"), True, False,
                               ["ones", "Xb"], [r_])
                            mm(p_[:, 0:512], ident, negm, False, True, ["CST"], [r_])
                            pR.append((p_, r_))
                        for hh in range(8):
                            p_, r_ = pR[hh // 4]
                            act(Lb[:, hh, :], p_[:, (hh % 4) * 128:(hh % 4 + 1) * 128], AF.Exp, [r_, sm], ["Lb"],
                                bias=SM[:, K_NACS, c, g * 8 + hh:g * 8 + hh + 1])
                        pC, rC = ps()
                        mm(pC[:, 0:128], BT[:, tsl], CTt[:, tsl], True, True, ["BT", "CT"], [rC])
                        act(cbs, pC[:, 0:128], AF.Copy, [rC], ["cbs"])
                        tt(Mb, Lb, cbs.unsqueeze(1).to_broadcast([128, 8, 128]), ALU.mult, ["Lb", "cbs"], ["Mb"])
                        tt(v3(xc), xsj, bh(SM[:, K_DT, c, :], g), ALU.mult, [("xstok", j), sm], ["xc"])
                        pYd, rYd = ps()
                        for hh in range(8):
                            mm(pYd[:, hh * 64:(hh + 1) * 64], Mb[:, hh, :], xc[:, hh * 64:(hh + 1) * 64], True, True,
                               ["Mb", "xc"], [rYd])
                        pYo, rYo = ps()
                        mm(pYo[:, 0:512], CTt[:, tsl], Sstbf[:, g, :], True, True, ["CT", ("Sstbf", g)], [rYo])
                        tt(v3(t1), v3(pYo[:, 0:512]), bh(SM[:, K_EA, c, :], g), ALU.mult, [rYo, sm], ["t1"])
                        tt(t1, t1, pYd[:, 0:512], ALU.add, ["t1", rYd], ["t1"])
                        tt(v3(t2), xsj, bh(dsk, g), ALU.mult, [("xstok", j), "dsk"], ["t2"])
                        tt(t1, t1, t2, ALU.add, ["t1", "t2"], ["t1"])
                        tt(yg, t1, zs[:, j, :], ALU.mult, ["t1", ("zs", j)], ["yg"])
                        act(junks, yg, AF.Square, ["yg"], ["junks", "sss"], accum_out=st1[:, 1:2])
                        rstd_from_ss(st1[:, 1:2], 512, "sss")
                        stt(mix[:, c, 1024 + g * 512:1024 + (g + 1) * 512], yg, st1[:, 1:2], snw[:, g * 512:(g + 1) * 512],
                            ALU.mult, ALU.mult, ["yg", "sss", "snw"], [("mix", c)])
                    if not (main and c == 7):
                        tt(v3(xcd), xsj, bh(SM[:, K_DTDS, c, :], g), ALU.mult, [("xstok", j), sm], ["xcd"])
                        pSt, rSt = ps()
                        mm(pSt[:, 0:512], Btok[:, j, :], xcd, True, True, ["Btok", "xcd"], [rSt])
                        Sg = Sst[:, g, :]
                        tt(v3(Sg), v3(Sg), bh(SM[:, K_CD, c, :], g), ALU.mult, [("Sst", g), sm], [("Sst", g)])
                        tt(Sg, Sg, pSt[:, 0:512], ALU.add, [("Sst", g), rSt], [("Sst", g)])
                        if (not main) and c == 7:
                            P.op("dve", lambda e, Sg=Sg: e.tensor_scalar_mul(out=Sg, in0=Sg, scalar1=flag),
                                 [("Sst", g), "CST"], [("Sst", g)])
                        act(Sstbf[:, g, :], Sg, AF.Copy, [("Sst", g)], [("Sstbf", g)])
        P.barrier()

    carve()
    h1 = alloc([8, 2048], F32)
    mixT = regA
    if dbg:
        for c in range(8):
            dma("sp", dbg_mix[c * 128:(c + 1) * 128, :], mix[:, c, :], [("mix", c)], ["dbgmix"])
    for c in range(8):
        dma("sp", h1[:, c, :], xm[c * 128:(c + 1) * 128, :], [], [("h1", c)])
        for half in range(2):
            pt, pres = ps()
            ptb = pt.bitcast(BF16)
            for k in range(8):
                fc = half * 8 + k
                tr(ptb[:, k * 128:(k + 1) * 128], mix[:, c, fc * 128:(fc + 1) * 128], identb, [("mix", c), "identb"], [pres])
            eng = "act" if half == 0 else "dve"
            cp(mixT[:, half * 8:(half + 1) * 8, c * 128:(c + 1) * 128], ptb[:, 0:1024].rearrange("p (k t) -> p k t", k=8),
               [pres], [("mixT", c)], eng=eng) if eng == "dve" else \
                act(mixT[:, half * 8:(half + 1) * 8, c * 128:(c + 1) * 128], ptb[:, 0:1024].rearrange("p (k t) -> p k t", k=8),
                    AF.Copy, [pres], [("mixT", c)])
    for s in range(8):
        Wo, wor = w_slab(w_out, 0, D, s * 256, 256)
        for c2 in range(4):
            p_, r_ = ps()
            for cc in range(2):
                c = c2 * 2 + cc
                for fc in range(16):
                    mm(p_[:, cc * 256:(cc + 1) * 256], mixT[:, fc, c * 128:(c + 1) * 128], Wo[:, fc, :], fc == 0, fc == 15,
                       [wor, ("mixT", c)], [r_])
            for cc in range(2):
                c = c2 * 2 + cc
                tt(h1[:, c, s * 256:(s + 1) * 256], h1[:, c, s * 256:(s + 1) * 256], p_[:, cc * 256:(cc + 1) * 256], ALU.add,
                   [r_, ("h1", c)], [("h1", c)])
    P.barrier()
    if dbg:
        for c in range(8):
            dma("sp", dbg_h1[c * 128:(c + 1) * 128, :], h1[:, c, :], [("h1", c)], ["dbgh1"])

    hnT = regB.rearrange("p a b -> p (a b)").rearrange("p (a b) -> p a b", a=16)
    uT = regA

    def load_h(c):
        return h1[:, c, :], ("h1", c)

    off[0] = s_off
    norm_transpose(load_h, None, nmw, "nmw", hnT, "hnT", "nh")
    rl = [alloc([512], F32) for _ in range(2)]
    assert off[0] <= s_off + 27000
    for fb in range(4):
        for s in range(8):
            Wu, wur = w_slab(w_up, 0, D, fb * 2048 + s * 256, 256)
            for f2 in range(2):
                fc = s * 2 + f2
                pp = [ps(), ps()]
                for dc in range(16):
                    for half in range(2):
                        mm(pp[half][0][:, 0:512], Wu[:, dc, f2 * 128:(f2 + 1) * 128], hnT[:, dc, half * 512:(half + 1) * 512],
                           dc == 0, dc == 15, [wur, "hnT"], [pp[half][1]])
                for half in range(2):
                    act(rl[half], pp[half][0][:, 0:512], AF.Relu, [pp[half][1]], [("rl", half)])
                    tt(uT[:, fc, half * 512:(half + 1) * 512], rl[half], rl[half], ALU.mult, [("rl", half)], [("uT", fc)])
        for db in range(4):
            WA, war = w_slab(w_down, fb * 2048, 1024, db * 512, 512)
            WB, wbr = w_slab(w_down, fb * 2048 + 1024, 1024, db * 512, 512)
            for c in range(8):
                p_, r_ = ps()
                for fc in range(16):
                    Wd, wdr = (WA, war) if fc < 8 else (WB, wbr)
                    mm(p_[:, 0:512], uT[:, fc, c * 128:(c + 1) * 128], Wd[:, fc % 8, :], fc == 0, fc == 15,
                       [wdr, ("uT", fc)], [r_])
                tt(h1[:, c, db * 512:(db + 1) * 512], h1[:, c, db * 512:(db + 1) * 512], p_[:, 0:512], ALU.add,
                   [r_, ("h1", c)], [("h1", c)])
    P.barrier()

    nfw = regA[:, 0:4, :].rearrange("p a b -> p (a b)").bitcast(F32)
    ob = [regB[:, 0:2, :].rearrange("p a b -> p (a b)").bitcast(F32),
          regB[:, 2:4, :].rearrange("p a b -> p (a b)").bitcast(F32)]
    junkf = regB[:, 4, :]
    dma("sp", nfw, norm_final_w.partition_broadcast(128), [], ["nfw"])
    for c in range(8):
        act(junkf, h1[:, c, :], AF.Square, [("h1", c)], ["junkf", ("ssf", c)], accum_out=st1[:, 2 + c:3 + c])
        rstd_from_ss(st1[:, 2 + c:3 + c], 2048, ("ssf", c))
        o = ob[c % 2]
        stt(o, h1[:, c, :], st1[:, 2 + c:3 + c], nfw, ALU.mult, ALU.mult, [("h1", c), ("ssf", c), "nfw"], [("ob", c % 2)])
        dma("sp", yout[c * 128:(c + 1) * 128, :], o, [("ob", c % 2)], ["yout"])
    P.op("sp", None, ["yout", "dbgmix", "dbgh1"], [])
    P.emit()
    es.close()
    return nc


def _consts(half):
    j = np.arange(128, dtype=np.float64)
    inv_freq = (10000.0 ** (-(np.arange(128, dtype=np.float32)) / np.float32(128))).astype(np.float32).astype(np.float64)
    rope = np.zeros((128, 2, 2, NT), np.float32)
    for pas in range(2):
        if half == 1:
            pos = np.arange(NT, dtype=np.float64) + pas * NT
        else:
            pos = np.arange(NT, dtype=np.float64)
        ang = (pos[None, :].astype(np.float32) * inv_freq[:, None].astype(np.float32)).astype(np.float32).astype(np.float64)
        rope[:, pas, 0, :] = np.cos(ang)
        rope[:, pas, 1, :] = np.sin(ang)
    cst = np.zeros((128, C_END), np.float32)
    idx = np.arange(128, dtype=np.float64)
    for h in range(4):
        lg = np.log(1.0 - 2.0 ** (-5.0 - h))
        rel = idx[None, :] - idx[:, None]
        dm = np.where(rel >= 0, np.exp(np.maximum(rel, 0) * lg), 0.0) * (256.0 ** -0.5)
        cst[:, C_DMASK + h * 128:C_DMASK + (h + 1) * 128] = dm
        cst[:, C_KD + h] = np.exp((127.0 - idx) * lg) * (256.0 ** -0.5)
        cst[:, C_QD + h] = np.exp((idx + 1.0) * lg)
    cst[:, C_TRI:C_TRI + 128] = (idx[:, None] <= idx[None, :]).astype(np.float32)
    neg = np.where(idx[None, :] >= idx[:, None], 0.0, -30000.0)
    cst[:, C_NEG:C_NEG + 512] = np.tile(neg, (1, 4))
    cst[:, C_ID:C_ID + 128] = np.eye(128)
    cst[:, C_FLAG] = float(half)
    return rope, cst


_NC_CACHE = {}


def kernel(x, norm_mix_w, w_in, ret_norm_w, conv_w, conv_b, dt_bias, a_log, d_skip, ssd_norm_w, w_out,
           norm_mlp_w, w_up, w_down, norm_final_w, _dbg=False):
    x = np.ascontiguousarray(np.asarray(x, dtype=np.float32))
    if _dbg not in _NC_CACHE:
        _NC_CACHE[_dbg] = build_program(dbg=_dbg)
    nc = _NC_CACHE[_dbg]
    shared = {
        "w_in": np.ascontiguousarray(w_in, dtype=np.float32), "w_out": np.ascontiguousarray(w_out, dtype=np.float32),
        "w_up": np.ascontiguousarray(w_up, dtype=np.float32), "w_down": np.ascontiguousarray(w_down, dtype=np.float32),
        "norm_mix_w": np.asarray(norm_mix_w, np.float32), "ret_norm_w": np.asarray(ret_norm_w, np.float32),
        "conv_w": np.asarray(conv_w, np.float32), "conv_b": np.asarray(conv_b, np.float32),
        "dt_bias": np.asarray(dt_bias, np.float32), "a_log": np.asarray(a_log, np.float32),
        "d_skip": np.asarray(d_skip, np.float32), "ssd_norm_w": np.asarray(ssd_norm_w, np.float32),
        "norm_mlp_w": np.asarray(norm_mlp_w, np.float32), "norm_final_w": np.asarray(norm_final_w, np.float32),
    }
    in_maps = []
    for core in range(8):
        b, half = core // 2, core % 2
        rope, cst = _consts(half)
        m = dict(shared)
        m["xm"] = np.ascontiguousarray(x[b, half * NT:(half + 1) * NT, :])
        m["xp"] = np.ascontiguousarray(x[b, 0:NT, :]) if half == 1 else np.zeros((NT, D), np.float32)
        m["rope"] = rope
        m["cst"] = cst
        in_maps.append(m)
    res = run_bass_kernel_spmd(nc, in_maps, core_ids=list(range(8)))
    out = np.zeros((4, 2048, D), np.float32)
    for core in range(8):
        b, half = core // 2, core % 2
        out[b, half * NT:(half + 1) * NT, :] = res.results[core]["y"]
    if _dbg:
        return out, res
    return out
```

```python
import contextlib
import numpy as np
import concourse.bass as bass
import concourse.mybir as mybir
from concourse.bass_utils import run_bass_kernel_spmd

F32 = mybir.dt.float32
BF16 = mybir.dt.bfloat16
U8 = mybir.dt.uint8
AF = mybir.ActivationFunctionType
ALU = mybir.AluOpType

NDSEM = 8
EPS = 1e-6
D = 2048
NT = 1024
O_Q, O_K, O_V, O_G, O_Z, O_X, O_B, O_C, O_DT = 0, 1024, 2048, 3072, 4096, 5120, 6144, 6400, 6656
IN_W = 6672
NSLAB = 5
C_DMASK, C_TRI, C_NEG, C_ID, C_KD, C_QD, C_FLAG, C_END = 0, 512, 640, 1152, 1280, 1284, 1288, 1289


class _St:
    __slots__ = ("w", "r")

    def __init__(self):
        self.w = None
        self.r = []


class Prog:
    ENG = ("pe", "act", "dve", "pool", "sp")

    def __init__(self, nc):
        self.nc = nc
        self.ops = {e: [] for e in self.ENG}
        self.res = {}
        self.ndma = {e: 0 for e in self.ENG}

    def _get(self, name):
        if name not in self.res:
            self.res[name] = {"whole": _St(), "parts": {}}
        return self.res[name]

    @staticmethod
    def _norm(r):
        if isinstance(r, tuple):
            return r[0], r[1]
        return r, None

    def op(self, eng, fn, reads=(), writes=(), dma=False):
        deps = set()
        me = ("dma", eng, self.ndma[eng]) if dma else ("op", eng, len(self.ops[eng]))
        for r in reads:
            name, idx = self._norm(r)
            e = self._get(name)
            if e["whole"].w:
                deps.add(e["whole"].w)
            if idx is None:
                for p in e["parts"].values():
                    if p.w:
                        deps.add(p.w)
            else:
                p = e["parts"].get(idx)
                if p is not None and p.w:
                    deps.add(p.w)
        for r in writes:
            name, idx = self._norm(r)
            e = self._get(name)
            wh = e["whole"]
            if wh.w:
                deps.add(wh.w)
            deps.update(wh.r)
            if idx is None:
                for p in e["parts"].values():
                    if p.w:
                        deps.add(p.w)
                    deps.update(p.r)
            else:
                p = e["parts"].get(idx)
                if p is not None:
                    if p.w:
                        deps.add(p.w)
                    deps.update(p.r)
        for r in reads:
            name, idx = self._norm(r)
            e = self._get(name)
            if idx is None:
                e["whole"].r.append(me)
            else:
                e["parts"].setdefault(idx, _St()).r.append(me)
        for r in writes:
            name, idx = self._norm(r)
            e = self._get(name)
            st = _St()
            st.w = me
            if idx is None:
                e["whole"] = st
                e["parts"] = {}
            else:
                e["parts"][idx] = st
        deps.discard(me)
        if eng == "pe":
            deps = {d for d in deps if not (d[0] == "op" and d[1] == "pe")}
        self._add(eng, fn, deps, dma)
        return me

    def _add(self, eng, fn, deps, dma):
        rec = {"fn": fn, "deps": deps, "dma": dma, "signal": False,
               "dma_idx": self.ndma[eng] if dma else None}
        if dma:
            self.ndma[eng] += 1
        for d in deps:
            if d[0] == "op":
                self.ops[d[1]][d[2]]["signal"] = True
        self.ops[eng].append(rec)

    def barrier(self):
        deps = set()
        for e in self.ENG:
            for i in range(len(self.ops[e]) - 1, -1, -1):
                if not self.ops[e][i]["dma"] and self.ops[e][i]["fn"] is not None:
                    deps.add(("op", e, i))
                    break
            n = self.ndma[e]
            for i in range(max(0, n - NDSEM), n):
                deps.add(("dma", e, i))
        for e in self.ENG:
            d = {x for x in deps if not (x[0] == "op" and x[1] == e and e == "pe")}
            self._add(e, None, d, False)
        self.res = {}

    def emit(self):
        nc = self.nc
        with contextlib.ExitStack() as es:
            sems = {e: es.enter_context(nc.semaphore("s_" + e)) for e in self.ENG if e != "sp"}
            dsems = {}
            for q in ("sp", "act", "pool"):
                if self.ndma[q]:
                    dsems[q] = [es.enter_context(nc.semaphore(f"d_{q}{i}")) for i in range(NDSEM)]
            for e in self.ENG:
                c = 0
                for rec in self.ops[e]:
                    if rec["signal"] and not rec["dma"]:
                        c += 1
                        rec["sig"] = c
            block = es.enter_context(nc.Block())
            ops = self.ops

            def run(e, eng):
                waited = {}

                def wait(key, sem, val):
                    if waited.get(key, 0) < val:
                        eng.wait_ge(sem, val)
                        waited[key] = val

                for rec in ops[e]:
                    for d in sorted(rec["deps"]):
                        if d[0] == "op":
                            wait(("op", d[1]), sems[d[1]], ops[d[1]][d[2]]["sig"])
                        else:
                            _, q, i = d
                            wait(("dma", q, i % NDSEM), dsems[q][i % NDSEM], 16 * (i // NDSEM + 1))
                    if rec["dma"]:
                        i = rec["dma_idx"]
                        if i >= NDSEM:
                            wait(("dma", e, i % NDSEM), dsems[e][i % NDSEM], 16 * (i // NDSEM))
                        rec["fn"](eng).then_inc(dsems[e][i % NDSEM], 16)
                    elif rec["fn"] is not None:
                        ins = rec["fn"](eng)
                        if rec["signal"]:
                            ins.then_inc(sems[e], 1)

            @block.tensor
            def _(eng):
                run("pe", eng)

            @block.scalar
            def _(eng):
                run("act", eng)

            @block.vector
            def _(eng):
                run("dve", eng)

            @block.gpsimd
            def _(eng):
                run("pool", eng)

            @block.sync
            def _(eng):
                run("sp", eng)


def build_program(dbg=False):
    nc = bass.Bass("TRN2", target_bir_lowering=False)

    def din(name, shape):
        return nc.dram_tensor(name, shape, F32, kind="ExternalInput").ap()

    xm = din("xm", [NT, D])
    xp = din("xp", [NT, D])
    w_in = din("w_in", [D, IN_W])
    w_out = din("w_out", [D, D])
    w_up = din("w_up", [D, 4 * D])
    w_down = din("w_down", [4 * D, D])
    norm_mix_w = din("norm_mix_w", [D])
    ret_norm_w = din("ret_norm_w", [1024])
    conv_w = din("conv_w", [4, 1536])
    conv_b = din("conv_b", [1536])
    dt_bias = din("dt_bias", [16])
    a_log = din("a_log", [16])
    d_skip = din("d_skip", [16])
    ssd_norm_w = din("ssd_norm_w", [1024])
    norm_mlp_w = din("norm_mlp_w", [D])
    norm_final_w = din("norm_final_w", [D])
    cst = din("cst", [128, C_END])
    rope = din("rope", [128, 2, 2, NT])
    yout = nc.dram_tensor("y", [NT, D], F32, kind="ExternalOutput").ap()
    if dbg:
        dbg_mix = nc.dram_tensor("dbg_mix", [NT, D], BF16, kind="ExternalOutput").ap()
        dbg_h1 = nc.dram_tensor("dbg_h1", [NT, D], F32, kind="ExternalOutput").ap()

    P = Prog(nc)
    es = contextlib.ExitStack()
    arena = es.enter_context(nc.sbuf_tensor("arena", [128, 206 * 1024], U8))
    off = [0]

    def alloc(shape, dt):
        sz = 4 if dt == F32 else 2
        n = int(np.prod(shape)) * sz
        v = arena[:, off[0]:off[0] + n].bitcast(dt)
        off[0] += (n + 63) // 64 * 64
        if len(shape) == 2:
            v = v.rearrange("p (a b) -> p a b", a=shape[0])
        elif len(shape) == 3:
            v = v.rearrange("p (a b c) -> p a b c", a=shape[0], b=shape[1])
        return v

    banks = [es.enter_context(nc.psum_tensor(f"ps{i}", [128, 512], F32)) for i in range(8)]
    psc = [0]

    def ps():
        i = psc[0] % 8
        psc[0] += 1
        return banks[i][:], f"ps{i}"

    CST = alloc([C_END], F32)
    slabs = [alloc([4096], BF16) for _ in range(NSLAB)]
    Wdt = alloc([16, 16], BF16)
    s_off = off[0]
    S = alloc([4, 512], F32)
    Sbf = alloc([512], BF16)
    Sst = alloc([2, 512], F32)
    Sstbf = alloc([2, 512], BF16)
    hist = alloc([12, 3], F32)
    SM = alloc([8, 8, 16], F32)
    K_DT, K_AC, K_ACS, K_AL, K_DTDS, K_EA, K_CD, K_NACS = range(8)
    rnw = alloc([1024], F32)
    snw = alloc([1024], F32)
    cw = alloc([4, 12], F32)
    cb = alloc([12], F32)
    nw = alloc([16], F32)
    nmw = alloc([16], F32)
    dtb = alloc([16], F32)
    abc = alloc([16], F32)
    dsk = alloc([16], F32)
    ones = alloc([128], F32)
    identb = alloc([128], BF16)
    st1 = alloc([16], F32)
    regA = alloc([16, 1024], BF16)
    regB = alloc([8, 2048], BF16)
    c_off = off[0]
    CBYTES = 64 * 1024
    assert c_off + CBYTES <= 206 * 1024, c_off

    def carve():
        off[0] = c_off

    dmask = CST[:, C_DMASK:C_DMASK + 512].rearrange("p (h c) -> p h c", h=4)
    tri = CST[:, C_TRI:C_TRI + 128]
    negm = CST[:, C_NEG:C_NEG + 512]
    ident = CST[:, C_ID:C_ID + 128]
    kd = CST[:, C_KD:C_KD + 4]
    qd = CST[:, C_QD:C_QD + 4]
    flag = CST[:, C_FLAG:C_FLAG + 1]
    gam = [1.0 - 2.0 ** (-5.0 - h) for h in range(4)]
    cdec = [float(np.float32(np.exp(np.float32(128.0) * np.log(np.float32(g))))) for g in gam]

    def dma(q, out, in_, reads, writes, slow=False):
        P.op(q, lambda e: e.dma_start(out=out, in_=in_, allow_slow_non_contiguous=slow), reads, writes, dma=True)

    def mm(out, lhsT, rhs, start, stop, reads, writes):
        P.op("pe", lambda e: e.matmul(out, lhsT=lhsT, rhs=rhs, start=start, stop=stop), reads, writes)

    def tr(out, in_, idn, reads, writes):
        P.op("pe", lambda e: e.transpose(out=out, in_=in_, identity=idn), reads, writes)

    def act(out, in_, func, reads, writes, **kw):
        P.op("act", lambda e: e.activation(out=out, in_=in_, func=func, **kw), reads, writes)

    def tt(out, in0, in1, op, reads, writes, eng="dve"):
        P.op(eng, lambda e: e.tensor_tensor(out=out, in0=in0, in1=in1, op=op), reads, writes)

    def tsc(out, in0, s1, s2, op0, op1, reads, writes, eng="dve"):
        P.op(eng, lambda e: e.tensor_scalar(out=out, in0=in0, scalar1=s1, scalar2=s2, op0=op0, op1=op1), reads, writes)

    def stt(out, in0, scalar, in1, op0, op1, reads, writes, eng="dve"):
        P.op(eng, lambda e: e.scalar_tensor_tensor(out=out, in0=in0, scalar=scalar, in1=in1, op0=op0, op1=op1), reads, writes)

    def cp(out, in_, reads, writes, eng="dve"):
        P.op(eng, lambda e: e.tensor_copy(out=out, in_=in_), reads, writes)

    slabc = [0]

    def slab_load(parts, shape):
        i = slabc[0] % NSLAB
        slabc[0] += 1
        kc, cols = shape
        v = slabs[i].rearrange("p (k c) -> p k c", k=kc)
        for (c0, c1, src) in parts:
            dma("pool", v[:, :, c0:c1], src.rearrange("(k p) c -> p k c", p=128), [], [f"slab{i}"])
        return v, f"slab{i}"

    def w_slab(w, r0, rows, c0, cols):
        return slab_load([(0, cols, w[r0:r0 + rows, c0:c0 + cols])], (rows // 128, cols))

    def rstd_from_ss(ss_ap, n, res):
        act(ss_ap, ss_ap, AF.Ln, [res, "st1"], [res], scale=1.0 / n, bias=epsb[:, 0:1])
        act(ss_ap, ss_ap, AF.Exp, [res], [res], scale=-0.5)

    epsb = alloc([1], F32) if False else st1[:, 15:16]
    dma("sp", CST, cst, [], ["CST"])
    P.op("dve", lambda e: e.memset(st1, EPS), [], ["st1"])
    P.op("dve", lambda e: e.memset(ones, 1.0), [], ["ones"])
    P.op("dve", lambda e: e.memset(S, 0.0), [], ["S"])
    P.op("dve", lambda e: e.memset(Sst, 0.0), [], ["Sst"])
    P.op("dve", lambda e: e.memset(Sstbf, 0.0), [], ["Sstbf"])
    P.op("dve", lambda e: e.memset(hist, 0.0), [], ["hist"])
    cp(identb, ident, ["CST"], ["identb"])
    dma("sp", rnw, ret_norm_w.partition_broadcast(128), [], ["rnw"])
    dma("sp", snw, ssd_norm_w.partition_broadcast(128), [], ["snw"])
    dma("sp", dtb, dt_bias.partition_broadcast(128), [], ["dtb"])
    dma("sp", abc, a_log.partition_broadcast(128), [], ["abc"])
    dma("sp", dsk, d_skip.partition_broadcast(128), [], ["dsk"])
    dma("sp", nw, norm_mix_w.rearrange("(c p) -> p c", p=128), [], ["nw"], slow=True)
    dma("sp", nmw, norm_mlp_w.rearrange("(c p) -> p c", p=128), [], ["nmw"], slow=True)
    dma("sp", cb, conv_b.rearrange("(c p) -> p c", p=128), [], ["cb"], slow=True)
    for jj in range(4):
        dma("sp", cw[:, jj, :], conv_w[jj, :].rearrange("(c p) -> p c", p=128), [], ["cw"], slow=True)
    act(abc, abc, AF.Exp, ["abc"], ["abc"])
    tsc(abc, abc, -1.0, None, ALU.mult, ALU.bypass, ["abc"], ["abc"]) if False else \
        P.op("dve", lambda e: e.tensor_scalar_mul(out=abc, in0=abc, scalar1=-1.0), ["abc"], ["abc"])
    dma("pool", Wdt, w_in[:, O_DT:O_DT + 16].rearrange("(k p) c -> p k c", p=128), [], ["Wdt"])

    xnT = regA
    mix = regB

    def norm_transpose(load_fn, src_res, wvec, wres, dstT, dst_res, tag):
        carve_base = off[0]
        junk = alloc([2048], BF16)
        xb = [alloc([2048], BF16) for _ in range(2)]
        ssq = alloc([8], F32)
        for c in range(8):
            xt, xres = load_fn(c)
            act(junk, xt, AF.Square, [xres], [tag + "junk", (tag + "ss", c)], accum_out=ssq[:, c:c + 1])
            rstd_from_ss(ssq[:, c:c + 1], 2048, (tag + "ss", c))
            xbc = xb[c % 2]
            tsc(xbc, xt, ssq[:, c:c + 1], None, ALU.mult, ALU.bypass, [xres, (tag + "ss", c)], [(tag + "xb", c % 2)]) if False else \
                P.op("dve", lambda e, xbc=xbc, xt=xt, c=c: e.tensor_scalar_mul(out=xbc, in0=xt, scalar1=ssq[:, c:c + 1]),
                     [xres, (tag + "ss", c)], [(tag + "xb", c % 2)])
            for half in range(2):
                pt, pres = ps()
                ptb = pt.bitcast(BF16)
                for k in range(8):
                    dc = half * 8 + k
                    tr(ptb[:, k * 128:(k + 1) * 128], xbc[:, dc * 128:(dc + 1) * 128], identb,
                       [(tag + "xb", c % 2), "identb"], [pres])
                tt(dstT[:, half * 8:(half + 1) * 8, c * 128:(c + 1) * 128],
                   ptb[:, 0:1024].rearrange("p (k t) -> p k t", k=8),
                   wvec[:, half * 8:(half + 1) * 8].unsqueeze(2).to_broadcast([128, 8, 128]),
                   ALU.mult, [pres, wres], [(dst_res, c)])
        off[0] = carve_base

    for pas in range(2):
        main = pas == 1
        xsrc = xm if main else xp
        carve()
        xbuf = [alloc([2048], F32) for _ in range(2)]

        def load_x(c, xsrc=xsrc, xbuf=xbuf):
            b = xbuf[c % 2]
            dma("sp", b, xsrc[c * 128:(c + 1) * 128, :], [], [("xbuf", c % 2)])
            return b, ("xbuf", c % 2)

        norm_transpose(load_x, None, nw, "nw", xnT, "xnT", "nx")
        P.barrier()

        def step(g):
            try:
                next(g)
                return True
            except StopIteration:
                return False

        def drive(tasks, prod, cons, ratio, pre=None):
            p0 = prod(*tasks[0], 0)
            if pre is not None:
                step(pre)
                for _ in range(3):
                    step(p0)
                while step(pre):
                    pass
            while step(p0):
                pass
            for i, tk in enumerate(tasks):
                cg = cons(*tk, i % 2)
                pg = prod(*tasks[i + 1], (i + 1) % 2) if i + 1 < len(tasks) else None
                pa = pg is not None
                ca = True
                cred = 0.0
                while ca:
                    ca = step(cg)
                    if pa:
                        cred += ratio
                        while pa and cred >= 1.0:
                            pa = step(pg)
                            cred -= 1.0
                while pa:
                    pa = step(pg)

        carve()
        cs = alloc([2, NT], F32)
        dma("sp", cs, rope[:, pas, :, :], [], ["cs"])
        kTr = [[alloc([512], BF16) for _ in range(2)] for _ in range(2)]
        qTr = [[alloc([512], BF16) for _ in range(2)] for _ in range(2)]
        ktok = [alloc([4, 256], BF16) for _ in range(2)]
        vtok = [alloc([4, 256], BF16) for _ in range(2)]
        gs = [alloc([4, 256], F32) for _ in range(2)]
        rt = [alloc([512], F32) for _ in range(2)]
        sT = alloc([128], BF16)
        tmpY = alloc([256], F32)
        yb = alloc([256], F32)
        yb2 = alloc([256], F32)
        junkr = alloc([256], BF16)
        assert off[0] <= c_off + CBYTES, off[0] - c_off

        def proj_rope(W, wres, t0, dst, dres):
            pk = []
            for a in range(2):
                p_, r_ = ps()
                for dc in range(16):
                    mm(p_[:, 0:512], W[:, dc, a * 128:(a + 1) * 128], xnT[:, dc, t0:t0 + 512], dc == 0, dc == 15,
                       [wres, "xnT"], [r_])
                pk.append((p_, r_))
            cosv = cs[:, 0, t0:t0 + 512]
            sinv = cs[:, 1, t0:t0 + 512]
            (p0, r0), (p1, r1) = pk
            tt(rt[0], p0[:, 0:512], cosv, ALU.mult, [r0, "cs"], ["rt0"])
            tt(rt[1], p1[:, 0:512], sinv, ALU.mult, [r1, "cs"], ["rt1"])
            tt(dst[0], rt[0], rt[1], ALU.subtract, ["rt0", "rt1"], [(dres, 0)])
            tt(rt[0], p0[:, 0:512], sinv, ALU.mult, [r0, "cs"], ["rt0"])
            tt(rt[1], p1[:, 0:512], cosv, ALU.mult, [r1, "cs"], ["rt1"])
            tt(dst[1], rt[0], rt[1], ALU.add, ["rt0", "rt1"], [(dres, 1)])

        def proj_tok_half(W, wres, t0, jj, func, dst, dres):
            p_, r_ = ps()
            for j2 in range(2):
                j = jj * 2 + j2
                for dc in range(16):
                    mm(p_[:, j2 * 256:(j2 + 1) * 256], xnT[:, dc, t0 + j * 128:t0 + (j + 1) * 128], W[:, dc, :],
                       dc == 0, dc == 15, [wres, "xnT"], [r_])
            act(dst[:, jj * 2:(jj + 1) * 2, :], p_[:, 0:512].rearrange("p (j c) -> p j c", j=2), func,
                [r_], [(dres, jj)])

        rslabs = {}

        def ret_producer(h, blk, par):
            if blk == 0:
                d_ = {"k": w_slab(w_in, 0, D, O_K + h * 256, 256), "v": w_slab(w_in, 0, D, O_V + h * 256, 256)}
                if main:
                    d_["q"] = w_slab(w_in, 0, D, O_Q + h * 256, 256)
                    d_["g"] = w_slab(w_in, 0, D, O_G + h * 256, 256)
                rslabs[h] = d_
            Wd = rslabs[h]
            t0 = blk * 512
            kT = kTr[par]
            proj_rope(Wd["k"][0], Wd["k"][1], t0, kT, f"kTr{par}")
            yield
            pt, pres = ps()
            ptb = pt.bitcast(BF16)
            for j in range(4):
                for a in range(2):
                    tr(ptb[:, j * 256 + a * 128:j * 256 + (a + 1) * 128], kT[a][:, j * 128:(j + 1) * 128], identb,
                       [(f"kTr{par}", a), "identb"], [pres])
            act(ktok[par], ptb[:, 0:1024].rearrange("p (j c) -> p j c", j=4), AF.Copy, [pres, "CST"], [f"ktok{par}"],
                scale=kd[:, h:h + 1])
            yield
            for jj in range(2):
                proj_tok_half(Wd["v"][0], Wd["v"][1], t0, jj, AF.Copy, vtok[par], f"vtok{par}")
                yield
            if main:
                proj_rope(Wd["q"][0], Wd["q"][1], t0, qTr[par], f"qTr{par}")
                yield
                for jj in range(2):
                    proj_tok_half(Wd["g"][0], Wd["g"][1], t0, jj, AF.Silu, gs[par], f"gs{par}")
                    yield

        def ret_consumer(h, blk, par):
            Sh = S[:, h, :]
            kT, qT, kt, vt, gsp = kTr[par], qTr[par], ktok[par], vtok[par], gs[par]
            if blk == 0:
                act(Sbf, Sh, AF.Copy, [("S", h)], ["Sbf"])
            for j in range(4):
                c = blk * 4 + j
                tsl = slice(j * 128, (j + 1) * 128)
                vres = (f"vtok{par}", j // 2)
                if main:
                    pS, rS = ps()
                    for a in range(2):
                        mm(pS[:, 0:128], kT[a][:, tsl], qT[a][:, tsl], a == 0, a == 1,
                           [(f"kTr{par}", a), (f"qTr{par}", a)], [rS])
                    tt(sT, pS[:, 0:128], dmask[:, h, :], ALU.mult, [rS, "CST"], ["sT"])
                    yield
                    pY, rY = ps()
                    mm(pY[:, 0:256], sT, vt[:, j, :], True, True, ["sT", vres], [rY])
                    for a in range(2):
                        mm(pY[:, 256:512], qT[a][:, tsl], Sbf[:, a * 256:(a + 1) * 256], a == 0, a == 1,
                           [(f"qTr{par}", a), "Sbf"], [rY])
                    act(tmpY, pY[:, 256:512], AF.Copy, [rY, "CST"], ["tmpY"], scale=qd[:, h:h + 1])
                    tt(yb, pY[:, 0:256], tmpY, ALU.add, [rY, "tmpY"], ["yb"])
                    act(junkr, yb, AF.Square, ["yb"], ["junkr", "ssr"], accum_out=st1[:, 0:1])
                    rstd_from_ss(st1[:, 0:1], 256, "ssr")
                    stt(yb2, yb, st1[:, 0:1], rnw[:, h * 256:(h + 1) * 256], ALU.mult, ALU.mult,
                        ["yb", "ssr", "rnw"], ["yb2"])
                    tt(mix[:, c, h * 256:(h + 1) * 256], yb2, gsp[:, j, :], ALU.mult, ["yb2", (f"gs{par}", j // 2)],
                       [("mix", c)])
                    yield
                if not (main and c == 7):
                    pK, rK = ps()
                    for a in range(2):
                        mm(pK[:, a * 256:(a + 1) * 256], kt[:, j, a * 128:(a + 1) * 128], vt[:, j, :], True, True,
                           [f"ktok{par}", vres], [rK])
                    stt(Sh, Sh, cdec[h], pK[:, 0:512], ALU.mult, ALU.add, [rK, ("S", h)], [("S", h)])
                    act(Sbf, Sh, AF.Copy, [("S", h)], ["Sbf"])
                    yield

        drive([(h, blk) for h in range(4) for blk in range(2)], ret_producer, ret_consumer,
              (7.0 / 12.0) if main else 1.0)
        P.barrier()

        carve()
        BL = 256
        U = alloc([6, BL + 3], F32)
        accb = alloc([BL], F32)
        xsT = alloc([4, BL], F32)
        BT = [alloc([BL], BF16) for _ in range(2)]
        CTt = [alloc([BL], BF16) for _ in range(2)]
        xstok = [alloc([2, 512], F32) for _ in range(2)]
        Btok = [alloc([2, 128], BF16) for _ in range(2)]
        zs = [alloc([2, 512], F32) for _ in range(2)]
        Xb = alloc([8, 128], F32)
        Lb = alloc([8, 128], F32)
        cbs = alloc([128], F32)
        Mb = alloc([8, 128], BF16)
        xc = alloc([512], BF16)
        xcd = alloc([512], BF16)
        t1 = alloc([512], F32)
        t2 = alloc([512], F32)
        yg = alloc([512], F32)
        junks = alloc([512], BF16)
        tsm8 = alloc([8, 16], F32)
        assert off[0] <= c_off + CBYTES, off[0] - c_off

        def bh(ap16, g):
            return ap16[:, g * 8:(g + 1) * 8].unsqueeze(2).to_broadcast([128, 8, 64])

        def v3(ap512):
            return ap512.rearrange("p (h q) -> p h q", h=8)

        def c8(ap128):
            return ap128.rearrange("p (c h) -> p c h", c=8)

        def dt_batch():
            pD, rD = ps()
            for c in range(8):
                for dc in range(16):
                    mm(pD[:, c * 16:(c + 1) * 16], xnT[:, dc, c * 128:(c + 1) * 128], Wdt[:, dc, :], dc == 0, dc == 15,
                       ["Wdt", "xnT"], [rD])
            tt(tsm8, c8(pD[:, 0:128]), dtb.unsqueeze(1).to_broadcast([128, 8, 16]), ALU.add, [rD, "dtb"], ["tsm8"])
            act(tsm8, tsm8, AF.Exp, ["tsm8"], ["tsm8"])
            act(SM[:, K_DT], tsm8, AF.Ln, ["tsm8", "ones"], ["SM"], bias=ones[:, 0:1])
            tt(SM[:, K_AC], SM[:, K_DT], abc.unsqueeze(1).to_broadcast([128, 8, 16]), ALU.mult, ["SM", "abc"], ["SM"])
            yield
            pA, rA = ps()
            acf = SM[:, K_AC].rearrange("p c h -> p (c h)")
            mm(pA[:, 0:128], tri, acf, True, True, ["CST", "SM"], [rA])
            mm(pA[:, 128:256], ones, acf, True, True, ["ones", "SM"], [rA])
            act(SM[:, K_ACS], c8(pA[:, 0:128]), AF.Copy, [rA], ["SM"])
            act(SM[:, K_AL], c8(pA[:, 128:256]), AF.Copy, [rA], ["SM"])
            tt(tsm8, SM[:, K_AL], SM[:, K_ACS], ALU.subtract, ["SM"], ["tsm8"])
            act(tsm8, tsm8, AF.Exp, ["tsm8"], ["tsm8"])
            tt(SM[:, K_DTDS], tsm8, SM[:, K_DT], ALU.mult, ["tsm8", "SM"], ["SM"])
            act(SM[:, K_EA], SM[:, K_ACS], AF.Exp, ["SM"], ["SM"])
            act(SM[:, K_CD], SM[:, K_AL], AF.Exp, ["SM"], ["SM"])
            act(SM[:, K_NACS], SM[:, K_ACS], AF.Copy, ["SM"], ["SM"], scale=-1.0)
            yield

        sslabs = {}

        def ssd_producer(g, blk, par):
            if blk == 0:
                d_ = {"x0": w_slab(w_in, 0, D, O_X + g * 512, 256), "x1": w_slab(w_in, 0, D, O_X + g * 512 + 256, 256),
                      "bc": slab_load([(0, 128, w_in[:, O_B + g * 128:O_B + (g + 1) * 128]),
                                       (128, 256, w_in[:, O_C + g * 128:O_C + (g + 1) * 128])], (16, 256))}
                if main:
                    d_["z0"] = w_slab(w_in, 0, D, O_Z + g * 512, 256)
                    d_["z1"] = w_slab(w_in, 0, D, O_Z + g * 512 + 256, 256)
                sslabs[g] = d_
            Wd = sslabs[g]
            t0 = blk * BL
            for t in range(6):
                W, wr = Wd[("x0", "x0", "x1", "x1", "bc", "bc")[t]]
                co = (t % 2) * 128
                ch = (g * 4 + t) if t < 4 else (8 + g if t == 4 else 10 + g)
                pU, rU = ps()
                for dc in range(16):
                    mm(pU[:, 0:BL], W[:, dc, co:co + 128], xnT[:, dc, t0:t0 + BL], dc == 0, dc == 15, [wr, "xnT"], [rU])
                cp(U[:, t, 0:3], hist[:, ch, :], [("hist", ch)], [("U", t)])
                act(U[:, t, 3:BL + 3], pU[:, 0:BL], AF.Copy, [rU], [("U", t)])
                cp(hist[:, ch, :], U[:, t, BL:BL + 3], [("U", t)], [("hist", ch)])
                tsc(accb, U[:, t, 0:BL], cw[:, 0, ch:ch + 1], cb[:, ch:ch + 1], ALU.mult, ALU.add,
                    [("U", t), "cw", "cb"], ["acc"])
                for jj in range(1, 4):
                    stt(accb, U[:, t, jj:jj + BL], cw[:, jj, ch:ch + 1], accb, ALU.mult, ALU.add,
                        [("U", t), "cw", "acc"], ["acc"])
                if t < 4:
                    act(xsT[:, t, :], accb, AF.Silu, ["acc"], [("xsT", t)])
                elif t == 4:
                    act(BT[par], accb, AF.Silu, ["acc"], [f"BT{par}"])
                else:
                    act(CTt[par], accb, AF.Silu, ["acc"], [f"CT{par}"])
                yield
            for j in range(2):
                pX, rX = ps()
                for t in range(4):
                    tr(pX[:, t * 128:(t + 1) * 128], xsT[:, t, j * 128:(j + 1) * 128], ident, [("xsT", t), "CST"], [rX])
                act(xstok[par][:, j, :], pX[:, 0:512], AF.Copy, [rX], [(f"xstok{par}", j)])
            yield
            pB, rB = ps()
            pBb = pB.bitcast(BF16)
            for j in range(2):
                tr(pBb[:, j * 128:(j + 1) * 128], BT[par][:, j * 128:(j + 1) * 128], identb, [f"BT{par}", "identb"], [rB])
            cp(Btok[par], pBb[:, 0:256].rearrange("p (j c) -> p j c", j=2), [rB], [f"Btok{par}"])
            yield
            if main:
                for j in range(2):
                    pZ, rZ = ps()
                    for q, zk in enumerate(("z0", "z1")):
                        Wz, rz = Wd[zk]
                        for dc in range(16):
                            mm(pZ[:, q * 256:(q + 1) * 256], xnT[:, dc, t0 + j * 128:t0 + (j + 1) * 128], Wz[:, dc, :],
                               dc == 0, dc == 15, [rz, "xnT"], [rZ])
                    act(zs[par][:, j, :], pZ[:, 0:512], AF.Silu, [rZ], [(f"zs{par}", j)])
                    yield

        def ssd_consumer(g, blk, par):
            for j in range(2):
                c = blk * 2 + j
                tsl = slice(j * 128, (j + 1) * 128)
                xsj = v3(xstok[par][:, j, :])
                xres = (f"xstok{par}", j)
                if main:
                    tt(Xb, SM[:, K_AC, c, g * 8:(g + 1) * 8].unsqueeze(2).to_broadcast([128, 8, 128]),
                       tri.unsqueeze(1).to_broadcast([128, 8, 128]), ALU.mult, ["SM", "CST"], ["Xb"])
                    pR = []
                    for q in range(2):
                        p_, r_ = ps()
                        mm(p_[:, 0:512], ones, Xb[:, q * 4:(q + 1) * 4, :].rearrange("p h l -> p (h l)"), True, False,
                           ["ones", "Xb"], [r_])
                        mm(p_[:, 0:512], ident, negm, False, True, ["CST"], [r_])
                        pR.append((p_, r_))
                    for hh in range(8):
                        p_, r_ = pR[hh // 4]
                        act(Lb[:, hh, :], p_[:, (hh % 4) * 128:(hh % 4 + 1) * 128], AF.Exp, [r_, "SM"], ["Lb"],
                            bias=SM[:, K_NACS, c, g * 8 + hh:g * 8 + hh + 1])
                    pC, rC = ps()
                    mm(pC[:, 0:128], BT[par][:, tsl], CTt[par][:, tsl], True, True, [f"BT{par}", f"CT{par}"], [rC])
                    act(cbs, pC[:, 0:128], AF.Copy, [rC], ["cbs"])
                    yield
                    tt(Mb, Lb, cbs.unsqueeze(1).to_broadcast([128, 8, 128]), ALU.mult, ["Lb", "cbs"], ["Mb"])
                    tt(v3(xc), xsj, bh(SM[:, K_DT, c, :], g), ALU.mult, [xres, "SM"], ["xc"])
                    pYd, rYd = ps()
                    for hh in range(8):
                        mm(pYd[:, hh * 64:(hh + 1) * 64], Mb[:, hh, :], xc[:, hh * 64:(hh + 1) * 64], True, True,
                           ["Mb", "xc"], [rYd])
                    pYo, rYo = ps()
                    mm(pYo[:, 0:512], CTt[par][:, tsl], Sstbf[:, g, :], True, True, [f"CT{par}", ("Sstbf", g)], [rYo])
                    tt(v3(t1), v3(pYo[:, 0:512]), bh(SM[:, K_EA, c, :], g), ALU.mult, [rYo, "SM"], ["t1"])
                    tt(t1, t1, pYd[:, 0:512], ALU.add, ["t1", rYd], ["t1"])
                    tt(v3(t2), xsj, bh(dsk, g), ALU.mult, [xres, "dsk"], ["t2"])
                    tt(t1, t1, t2, ALU.add, ["t1", "t2"], ["t1"])
                    tt(yg, t1, zs[par][:, j, :], ALU.mult, ["t1", (f"zs{par}", j)], ["yg"])
                    act(junks, yg, AF.Square, ["yg"], ["junks", "sss"], accum_out=st1[:, 1:2])
                    rstd_from_ss(st1[:, 1:2], 512, "sss")
                    stt(mix[:, c, 1024 + g * 512:1024 + (g + 1) * 512], yg, st1[:, 1:2], snw[:, g * 512:(g + 1) * 512],
                        ALU.mult, ALU.mult, ["yg", "sss", "snw"], [("mix", c)])
                    yield
                if not (main and c == 7):
                    tt(v3(xcd), xsj, bh(SM[:, K_DTDS, c, :], g), ALU.mult, [xres, "SM"], ["xcd"])
                    pSt, rSt = ps()
                    mm(pSt[:, 0:512], Btok[par][:, j, :], xcd, True, True, [f"Btok{par}", "xcd"], [rSt])
                    Sg = Sst[:, g, :]
                    tt(v3(Sg), v3(Sg), bh(SM[:, K_CD, c, :], g), ALU.mult, [("Sst", g), "SM"], [("Sst", g)])
                    tt(Sg, Sg, pSt[:, 0:512], ALU.add, [("Sst", g), rSt], [("Sst", g)])
                    if (not main) and c == 7:
                        P.op("dve", lambda e, Sg=Sg: e.tensor_scalar_mul(out=Sg, in0=Sg, scalar1=flag),
                             [("Sst", g), "CST"], [("Sst", g)])
                    act(Sstbf[:, g, :], Sg, AF.Copy, [("Sst", g)], [("Sstbf", g)])
                    yield

        drive([(g, blk) for g in range(2) for blk in range(4)], ssd_producer, ssd_consumer,
              (10.0 / 6.0) if main else 4.0, pre=dt_batch())
        P.barrier()

    carve()
    h1 = alloc([8, 2048], F32)
    mixT = regA
    if dbg:
        for c in range(8):
            dma("sp", dbg_mix[c * 128:(c + 1) * 128, :], mix[:, c, :], [("mix", c)], ["dbgmix"])
    for c in range(8):
        dma("sp", h1[:, c, :], xm[c * 128:(c + 1) * 128, :], [], [("h1", c)])
        for half in range(2):
            pt, pres = ps()
            ptb = pt.bitcast(BF16)
            for k in range(8):
                fc = half * 8 + k
                tr(ptb[:, k * 128:(k + 1) * 128], mix[:, c, fc * 128:(fc + 1) * 128], identb, [("mix", c), "identb"], [pres])
            eng = "act" if half == 0 else "dve"
            cp(mixT[:, half * 8:(half + 1) * 8, c * 128:(c + 1) * 128], ptb[:, 0:1024].rearrange("p (k t) -> p k t", k=8),
               [pres], [("mixT", c)], eng=eng) if eng == "dve" else \
                act(mixT[:, half * 8:(half + 1) * 8, c * 128:(c + 1) * 128], ptb[:, 0:1024].rearrange("p (k t) -> p k t", k=8),
                    AF.Copy, [pres], [("mixT", c)])
    for s in range(8):
        Wo, wor = w_slab(w_out, 0, D, s * 256, 256)
        for c2 in range(4):
            p_, r_ = ps()
            for cc in range(2):
                c = c2 * 2 + cc
                for fc in range(16):
                    mm(p_[:, cc * 256:(cc + 1) * 256], mixT[:, fc, c * 128:(c + 1) * 128], Wo[:, fc, :], fc == 0, fc == 15,
                       [wor, ("mixT", c)], [r_])
            for cc in range(2):
                c = c2 * 2 + cc
                tt(h1[:, c, s * 256:(s + 1) * 256], h1[:, c, s * 256:(s + 1) * 256], p_[:, cc * 256:(cc + 1) * 256], ALU.add,
                   [r_, ("h1", c)], [("h1", c)])
    P.barrier()
    if dbg:
        for c in range(8):
            dma("sp", dbg_h1[c * 128:(c + 1) * 128, :], h1[:, c, :], [("h1", c)], ["dbgh1"])

    hnT = regB.rearrange("p a b -> p (a b)").rearrange("p (a b) -> p a b", a=16)
    uT = regA

    def load_h(c):
        return h1[:, c, :], ("h1", c)

    off[0] = s_off
    norm_transpose(load_h, None, nmw, "nmw", hnT, "hnT", "nh")
    rl = [alloc([512], F32) for _ in range(2)]
    assert off[0] <= s_off + 27000
    for fb in range(4):
        for s in range(8):
            Wu, wur = w_slab(w_up, 0, D, fb * 2048 + s * 256, 256)
            for f2 in range(2):
                fc = s * 2 + f2
                pp = [ps(), ps()]
                for dc in range(16):
                    for half in range(2):
                        mm(pp[half][0][:, 0:512], Wu[:, dc, f2 * 128:(f2 + 1) * 128], hnT[:, dc, half * 512:(half + 1) * 512],
                           dc == 0, dc == 15, [wur, "hnT"], [pp[half][1]])
                for half in range(2):
                    act(rl[half], pp[half][0][:, 0:512], AF.Relu, [pp[half][1]], [("rl", half)])
                    tt(uT[:, fc, half * 512:(half + 1) * 512], rl[half], rl[half], ALU.mult, [("rl", half)], [("uT", fc)])
        for db in range(4):
            WA, war = w_slab(w_down, fb * 2048, 1024, db * 512, 512)
            WB, wbr = w_slab(w_down, fb * 2048 + 1024, 1024, db * 512, 512)
            for c in range(8):
                p_, r_ = ps()
                for fc in range(16):
                    Wd, wdr = (WA, war) if fc < 8 else (WB, wbr)
                    mm(p_[:, 0:512], uT[:, fc, c * 128:(c + 1) * 128], Wd[:, fc % 8, :], fc == 0, fc == 15,
                       [wdr, ("uT", fc)], [r_])
                tt(h1[:, c, db * 512:(db + 1) * 512], h1[:, c, db * 512:(db + 1) * 512], p_[:, 0:512], ALU.add,
                   [r_, ("h1", c)], [("h1", c)])
    P.barrier()

    nfw = regA[:, 0:4, :].rearrange("p a b -> p (a b)").bitcast(F32)
    ob = [regB[:, 0:2, :].rearrange("p a b -> p (a b)").bitcast(F32),
          regB[:, 2:4, :].rearrange("p a b -> p (a b)").bitcast(F32)]
    junkf = regB[:, 4, :]
    dma("sp", nfw, norm_final_w.partition_broadcast(128), [], ["nfw"])
    for c in range(8):
        act(junkf, h1[:, c, :], AF.Square, [("h1", c)], ["junkf", ("ssf", c)], accum_out=st1[:, 2 + c:3 + c])
        rstd_from_ss(st1[:, 2 + c:3 + c], 2048, ("ssf", c))
        o = ob[c % 2]
        stt(o, h1[:, c, :], st1[:, 2 + c:3 + c], nfw, ALU.mult, ALU.mult, [("h1", c), ("ssf", c), "nfw"], [("ob", c % 2)])
        dma("sp", yout[c * 128:(c + 1) * 128, :], o, [("ob", c % 2)], ["yout"])
    P.op("sp", None, ["yout", "dbgmix", "dbgh1"], [])
    P.emit()
    es.close()
    return nc


def _consts(half):
    j = np.arange(128, dtype=np.float64)
    inv_freq = (10000.0 ** (-(np.arange(128, dtype=np.float32)) / np.float32(128))).astype(np.float32).astype(np.float64)
    rope = np.zeros((128, 2, 2, NT), np.float32)
    for pas in range(2):
        if half == 1:
            pos = np.arange(NT, dtype=np.float64) + pas * NT
        else:
            pos = np.arange(NT, dtype=np.float64)
        ang = (pos[None, :].astype(np.float32) * inv_freq[:, None].astype(np.float32)).astype(np.float32).astype(np.float64)
        rope[:, pas, 0, :] = np.cos(ang)
        rope[:, pas, 1, :] = np.sin(ang)
    cst = np.zeros((128, C_END), np.float32)
    idx = np.arange(128, dtype=np.float64)
    for h in range(4):
        lg = np.log(1.0 - 2.0 ** (-5.0 - h))
        rel = idx[None, :] - idx[:, None]
        dm = np.where(rel >= 0, np.exp(np.maximum(rel, 0) * lg), 0.0) * (256.0 ** -0.5)
        cst[:, C_DMASK + h * 128:C_DMASK + (h + 1) * 128] = dm
        cst[:, C_KD + h] = np.exp((127.0 - idx) * lg) * (256.0 ** -0.5)
        cst[:, C_QD + h] = np.exp((idx + 1.0) * lg)
    cst[:, C_TRI:C_TRI + 128] = (idx[:, None] <= idx[None, :]).astype(np.float32)
    neg = np.where(idx[None, :] >= idx[:, None], 0.0, -30000.0)
    cst[:, C_NEG:C_NEG + 512] = np.tile(neg, (1, 4))
    cst[:, C_ID:C_ID + 128] = np.eye(128)
    cst[:, C_FLAG] = float(half)
    return rope, cst


_NC_CACHE = {}


def kernel(x, norm_mix_w, w_in, ret_norm_w, conv_w, conv_b, dt_bias, a_log, d_skip, ssd_norm_w, w_out,
           norm_mlp_w, w_up, w_down, norm_final_w, _dbg=False):
    x = np.ascontiguousarray(np.asarray(x, dtype=np.float32))
    if _dbg not in _NC_CACHE:
        _NC_CACHE[_dbg] = build_program(dbg=_dbg)
    nc = _NC_CACHE[_dbg]
    shared = {
        "w_in": np.ascontiguousarray(w_in, dtype=np.float32), "w_out": np.ascontiguousarray(w_out, dtype=np.float32),
        "w_up": np.ascontiguousarray(w_up, dtype=np.float32), "w_down": np.ascontiguousarray(w_down, dtype=np.float32),
        "norm_mix_w": np.asarray(norm_mix_w, np.float32), "ret_norm_w": np.asarray(ret_norm_w, np.float32),
        "conv_w": np.asarray(conv_w, np.float32), "conv_b": np.asarray(conv_b, np.float32),
        "dt_bias": np.asarray(dt_bias, np.float32), "a_log": np.asarray(a_log, np.float32),
        "d_skip": np.asarray(d_skip, np.float32), "ssd_norm_w": np.asarray(ssd_norm_w, np.float32),
        "norm_mlp_w": np.asarray(norm_mlp_w, np.float32), "norm_final_w": np.asarray(norm_final_w, np.float32),
    }
    in_maps = []
    for core in range(8):
        b, half = core // 2, core % 2
        rope, cst = _consts(half)
        m = dict(shared)
        m["xm"] = np.ascontiguousarray(x[b, half * NT:(half + 1) * NT, :])
        m["xp"] = np.ascontiguousarray(x[b, 0:NT, :]) if half == 1 else np.zeros((NT, D), np.float32)
        m["rope"] = rope
        m["cst"] = cst
        in_maps.append(m)
    res = run_bass_kernel_spmd(nc, in_maps, core_ids=list(range(8)))
    out = np.zeros((4, 2048, D), np.float32)
    for core in range(8):
        b, half = core // 2, core % 2
        out[b, half * NT:(half + 1) * NT, :] = res.results[core]["y"]
    if _dbg:
        return out, res
    return out
```

```python
import contextlib
import numpy as np
import concourse.bass as bass
import concourse.mybir as mybir
from concourse.bass_utils import run_bass_kernel_spmd

F32 = mybir.dt.float32
BF16 = mybir.dt.bfloat16
U8 = mybir.dt.uint8
AF = mybir.ActivationFunctionType
ALU = mybir.AluOpType

NDSEM = 8
EPS = 1e-6
D = 2048
NT = 1024
O_Q, O_K, O_V, O_G, O_Z, O_X, O_B, O_C, O_DT = 0, 1024, 2048, 3072, 4096, 5120, 6144, 6400, 6656
IN_W = 6672
NSLAB = 5
C_DMASK, C_TRI, C_NEG, C_ID, C_KD, C_QD, C_FLAG, C_END = 0, 512, 640, 1152, 1280, 1284, 1288, 1289


class _St:
    __slots__ = ("w", "r")

    def __init__(self):
        self.w = None
        self.r = []


SCHEDULE = True
VERBOSE = False
XLAT = 350.0


class Prog:
    ENG = ("pe", "act", "dve", "pool", "sp")

    def __init__(self, nc):
        self.nc = nc
        self.recs = []
        self.res = {}
        self.seg = 0
        self.segs = [[]]
        self.ops = {e: [] for e in self.ENG}
        self.ndma = {e: 0 for e in self.ENG}

    def _get(self, name):
        if name not in self.res:
            self.res[name] = {"whole": _St(), "parts": {}}
        return self.res[name]

    @staticmethod
    def _norm(r):
        if isinstance(r, tuple):
            return r[0], r[1]
        return r, None

    def op(self, eng, fn, reads=(), writes=(), dma=False, cost=300.0, lat=0.0):
        deps = set()
        me = len(self.recs)
        for r in reads:
            name, idx = self._norm(r)
            e = self._get(name)
            if e["whole"].w is not None:
                deps.add(e["whole"].w)
            if idx is None:
                for p in e["parts"].values():
                    if p.w is not None:
                        deps.add(p.w)
            else:
                p = e["parts"].get(idx)
                if p is not None and p.w is not None:
                    deps.add(p.w)
        for r in writes:
            name, idx = self._norm(r)
            e = self._get(name)
            wh = e["whole"]
            if wh.w is not None:
                deps.add(wh.w)
            deps.update(wh.r)
            if idx is None:
                for p in e["parts"].values():
                    if p.w is not None:
                        deps.add(p.w)
                    deps.update(p.r)
            else:
                p = e["parts"].get(idx)
                if p is not None:
                    if p.w is not None:
                        deps.add(p.w)
                    deps.update(p.r)
        for r in reads:
            name, idx = self._norm(r)
            e = self._get(name)
            if idx is None:
                e["whole"].r.append(me)
            else:
                e["parts"].setdefault(idx, _St()).r.append(me)
        for r in writes:
            name, idx = self._norm(r)
            e = self._get(name)
            st = _St()
            st.w = me
            if idx is None:
                e["whole"] = st
                e["parts"] = {}
            else:
                e["parts"][idx] = st
        deps.discard(me)
        rec = {"id": me, "eng": eng, "fn": fn, "deps": deps, "dma": dma, "signal": False,
               "cost": cost, "lat": lat, "barrier": False}
        self.recs.append(rec)
        self.segs[-1].append(rec)
        return me

    def barrier(self):
        self.segs.append([])
        self.res = {}

    def _schedule(self, seg, efree):
        ids = {r["id"] for r in seg}
        succ = {r["id"]: [] for r in seg}
        indeg = {}
        rtime = {}
        for r in seg:
            d = [x for x in r["deps"] if x in ids]
            indeg[r["id"]] = len(d)
            rtime[r["id"]] = 0.0
            for x in d:
                succ[x].append(r["id"])
        ready = {e: [] for e in self.ENG}
        inord = ("pool", "sp") if SCHEDULE is True else self.ENG
        pend_dma = {e: [r["id"] for r in seg if r["eng"] == e] for e in inord}
        dma_pos = {e: 0 for e in inord}
        for r in seg:
            if indeg[r["id"]] == 0:
                ready[r["eng"]].append(r["id"])
        order = {e: [] for e in self.ENG}
        fin = {}
        n_left = len(seg)
        recs = self.recs
        while n_left:
            best = None
            for e in self.ENG:
                rl = ready[e]
                if not rl:
                    continue
                if e in inord:
                    nxt = pend_dma[e][dma_pos[e]]
                    if nxt not in rl:
                        continue
                    cand = nxt
                    st = max(efree[e], rtime[cand])
                else:
                    t = efree[e]
                    cand = None
                    cst = None
                    for i in rl:
                        s_ = max(t, rtime[i])
                        if cand is None or s_ < cst - 1e-9 or (abs(s_ - cst) <= 1e-9 and i < cand):
                            cand, cst = i, s_
                    st = cst
                if best is None or st < best[0] - 1e-9 or (abs(st - best[0]) <= 1e-9 and cand < best[2]):
                    best = (st, e, cand)
            assert best is not None, "scheduler deadlock"
            st, e, i = best
            r = recs[i]
            ready[e].remove(i)
            if e in inord:
                dma_pos[e] += 1
            efree[e] = st + r["cost"]
            fin[i] = st + r["cost"] + r["lat"]
            order[e].append(r)
            n_left -= 1
            for s in succ[i]:
                lat = 0.0 if recs[s]["eng"] == e and not r["dma"] else XLAT
                rtime[s] = max(rtime[s], fin[i] + lat)
                indeg[s] -= 1
                if indeg[s] == 0:
                    ready[recs[s]["eng"]].append(s)
        return order

    def finalize(self):
        efree = {e: 0.0 for e in self.ENG}
        nseg = len(self.segs)
        for si, seg in enumerate(self.segs):
            if SCHEDULE:
                order = self._schedule(seg, efree)
                if VERBOSE:
                    busy = {e: round(sum(r["cost"] for r in seg if r["eng"] == e) / 1e3, 1) for e in self.ENG}
                    print("seg", si, len(seg), "end", round(max(efree.values()) / 1e3, 1), "busy", busy)
            else:
                order = {e: [r for r in seg if r["eng"] == e] for e in self.ENG}
            for e in self.ENG:
                self.ops[e].extend(order[e])
            if si + 1 < nseg:
                deps = set()
                for e in self.ENG:
                    for r in reversed(self.ops[e]):
                        if not r["dma"] and r["fn"] is not None:
                            deps.add(r["id"])
                            break
                    dm = [r["id"] for r in self.ops[e] if r["dma"]]
                    deps.update(dm[-NDSEM:])
                t = max(efree.values()) + 2000.0
                for e in self.ENG:
                    d = {x for x in deps if not (self.recs[x]["eng"] == "pe" and e == "pe" and not self.recs[x]["dma"])}
                    rec = {"id": len(self.recs), "eng": e, "fn": None, "deps": d, "dma": False, "signal": False,
                           "cost": 0.0, "lat": 0.0, "barrier": True}
                    self.recs.append(rec)
                    self.ops[e].append(rec)
                    efree[e] = t
        self.sim_ns = max(efree.values())

    def emit(self):
        self.finalize()
        nc = self.nc
        recs = self.recs
        for e in self.ENG:
            for r in self.ops[e]:
                for d in r["deps"]:
                    dr = recs[d]
                    if dr["dma"]:
                        continue
                    if dr["eng"] == "pe" and e == "pe":
                        continue
                    dr["signal"] = True
        with contextlib.ExitStack() as es:
            sems = {e: es.enter_context(nc.semaphore("s_" + e)) for e in self.ENG if e != "sp"}
            dsems = {}
            for q in ("sp", "act", "pool"):
                if any(r["dma"] for r in self.ops[q]):
                    dsems[q] = [es.enter_context(nc.semaphore(f"d_{q}{i}")) for i in range(NDSEM)]
            for e in self.ENG:
                c = 0
                k = 0
                for rec in self.ops[e]:
                    if rec["dma"]:
                        rec["dma_idx"] = k
                        k += 1
                    elif rec["signal"]:
                        c += 1
                        rec["sig"] = c
            block = es.enter_context(nc.Block())
            ops = self.ops

            def run(e, eng):
                waited = {}

                def wait(key, sem, val):
                    if waited.get(key, 0) < val:
                        eng.wait_ge(sem, val)
                        waited[key] = val

                for rec in ops[e]:
                    for d in sorted(rec["deps"]):
                        dr = recs[d]
                        if dr["dma"]:
                            q, i = dr["eng"], dr["dma_idx"]
                            wait(("dma", q, i % NDSEM), dsems[q][i % NDSEM], 16 * (i // NDSEM + 1))
                        elif not (dr["eng"] == "pe" and e == "pe"):
                            wait(("op", dr["eng"]), sems[dr["eng"]], dr["sig"])
                    if rec["dma"]:
                        i = rec["dma_idx"]
                        if i >= NDSEM:
                            wait(("dma", e, i % NDSEM), dsems[e][i % NDSEM], 16 * (i // NDSEM))
                        rec["fn"](eng).then_inc(dsems[e][i % NDSEM], 16)
                    elif rec["fn"] is not None:
                        ins = rec["fn"](eng)
                        if rec["signal"]:
                            ins.then_inc(sems[e], 1)

            @block.tensor
            def _(eng):
                run("pe", eng)

            @block.scalar
            def _(eng):
                run("act", eng)

            @block.vector
            def _(eng):
                run("dve", eng)

            @block.gpsimd
            def _(eng):
                run("pool", eng)

            @block.sync
            def _(eng):
                run("sp", eng)


def build_program(dbg=False):
    nc = bass.Bass("TRN2", target_bir_lowering=False)

    def din(name, shape):
        return nc.dram_tensor(name, shape, F32, kind="ExternalInput").ap()

    xm = din("xm", [NT, D])
    xp = din("xp", [NT, D])
    w_in = din("w_in", [D, IN_W])
    w_out = din("w_out", [D, D])
    w_up = din("w_up", [D, 4 * D])
    w_down = din("w_down", [4 * D, D])
    norm_mix_w = din("norm_mix_w", [D])
    ret_norm_w = din("ret_norm_w", [1024])
    conv_w = din("conv_w", [4, 1536])
    conv_b = din("conv_b", [1536])
    dt_bias = din("dt_bias", [16])
    a_log = din("a_log", [16])
    d_skip = din("d_skip", [16])
    ssd_norm_w = din("ssd_norm_w", [1024])
    norm_mlp_w = din("norm_mlp_w", [D])
    norm_final_w = din("norm_final_w", [D])
    cst = din("cst", [128, C_END])
    rope = din("rope", [128, 2, 2, NT])
    yout = nc.dram_tensor("y", [NT, D], F32, kind="ExternalOutput").ap()
    if dbg:
        dbg_mix = nc.dram_tensor("dbg_mix", [NT, D], BF16, kind="ExternalOutput").ap()
        dbg_h1 = nc.dram_tensor("dbg_h1", [NT, D], F32, kind="ExternalOutput").ap()

    P = Prog(nc)
    es = contextlib.ExitStack()
    arena = es.enter_context(nc.sbuf_tensor("arena", [128, 206 * 1024], U8))
    off = [0]

    def alloc(shape, dt):
        sz = 4 if dt == F32 else 2
        n = int(np.prod(shape)) * sz
        v = arena[:, off[0]:off[0] + n].bitcast(dt)
        off[0] += (n + 63) // 64 * 64
        if len(shape) == 2:
            v = v.rearrange("p (a b) -> p a b", a=shape[0])
        elif len(shape) == 3:
            v = v.rearrange("p (a b c) -> p a b c", a=shape[0], b=shape[1])
        return v

    banks = [es.enter_context(nc.psum_tensor(f"ps{i}", [128, 512], F32)) for i in range(8)]
    psc = [0]

    def ps():
        i = psc[0] % 8
        psc[0] += 1
        return banks[i][:], f"ps{i}"

    CST = alloc([C_END], F32)
    slabs = [alloc([4096], BF16) for _ in range(NSLAB)]
    Wdt = alloc([16, 16], BF16)
    s_off = off[0]
    S = alloc([4, 512], F32)
    Sbf = alloc([512], BF16)
    Sst = alloc([2, 512], F32)
    Sstbf = alloc([2, 512], BF16)
    hist = alloc([12, 3], F32)
    SM = alloc([8, 8, 16], F32)
    K_DT, K_AC, K_ACS, K_AL, K_DTDS, K_EA, K_CD, K_NACS = range(8)
    rnw = alloc([1024], F32)
    snw = alloc([1024], F32)
    cw = alloc([4, 12], F32)
    cb = alloc([12], F32)
    nw = alloc([16], F32)
    nmw = alloc([16], F32)
    dtb = alloc([16], F32)
    abc = alloc([16], F32)
    dsk = alloc([16], F32)
    ones = alloc([128], F32)
    identb = alloc([128], BF16)
    st1 = alloc([16], F32)
    regA = alloc([16, 1024], BF16)
    regB = alloc([8, 2048], BF16)
    c_off = off[0]
    CBYTES = 64 * 1024
    assert c_off + CBYTES <= 206 * 1024, c_off

    def carve():
        off[0] = c_off

    dmask = CST[:, C_DMASK:C_DMASK + 512].rearrange("p (h c) -> p h c", h=4)
    tri = CST[:, C_TRI:C_TRI + 128]
    negm = CST[:, C_NEG:C_NEG + 512]
    ident = CST[:, C_ID:C_ID + 128]
    kd = CST[:, C_KD:C_KD + 4]
    qd = CST[:, C_QD:C_QD + 4]
    flag = CST[:, C_FLAG:C_FLAG + 1]
    gam = [1.0 - 2.0 ** (-5.0 - h) for h in range(4)]
    cdec = [float(np.float32(np.exp(np.float32(128.0) * np.log(np.float32(g))))) for g in gam]

    def dma(q, out, in_, reads, writes, slow=False):
        nb = 128.0 * float(np.prod(out.shape[1:])) * (4 if out.dtype == F32 else 2)
        P.op(q, lambda e: e.dma_start(out=out, in_=in_, allow_slow_non_contiguous=slow), reads, writes, dma=True,
             cost=100.0, lat=2500.0 + nb / 150.0)

    def mm(out, lhsT, rhs, start, stop, reads, writes):
        c_ = max(64.0, float(out.shape[-1]) / 2.4 + 3.0) * (4.0 if lhsT.dtype == F32 else 1.0)
        P.op("pe", lambda e: e.matmul(out, lhsT=lhsT, rhs=rhs, start=start, stop=stop), reads, writes, cost=c_)

    def tr(out, in_, idn, reads, writes):
        P.op("pe", lambda e: e.transpose(out=out, in_=in_, identity=idn), reads, writes,
             cost=(280.0 if in_.dtype == F32 else 70.0))

    def act(out, in_, func, reads, writes, **kw):
        P.op("act", lambda e: e.activation(out=out, in_=in_, func=func, **kw), reads, writes,
             cost=224.0 + 0.75 * float(np.prod(in_.shape[1:])))

    def tt(out, in0, in1, op, reads, writes, eng="dve"):
        P.op(eng, lambda e: e.tensor_tensor(out=out, in0=in0, in1=in1, op=op), reads, writes,
             cost=70.0 + 1.05 * float(np.prod(out.shape[1:])))

    def tsc(out, in0, s1, s2, op0, op1, reads, writes, eng="dve"):
        P.op(eng, lambda e: e.tensor_scalar(out=out, in0=in0, scalar1=s1, scalar2=s2, op0=op0, op1=op1), reads, writes,
             cost=70.0 + 0.6 * float(np.prod(out.shape[1:])))

    def stt(out, in0, scalar, in1, op0, op1, reads, writes, eng="dve"):
        P.op(eng, lambda e: e.scalar_tensor_tensor(out=out, in0=in0, scalar=scalar, in1=in1, op0=op0, op1=op1), reads, writes,
             cost=70.0 + 1.05 * float(np.prod(out.shape[1:])))

    def cp(out, in_, reads, writes, eng="dve"):
        P.op(eng, lambda e: e.tensor_copy(out=out, in_=in_), reads, writes,
             cost=70.0 + 0.6 * float(np.prod(out.shape[1:])))

    slabc = [0]

    def slab_load(parts, shape):
        i = slabc[0] % NSLAB
        slabc[0] += 1
        kc, cols = shape
        v = slabs[i].rearrange("p (k c) -> p k c", k=kc)
        for (c0, c1, src) in parts:
            dma("pool", v[:, :, c0:c1], src.rearrange("(k p) c -> p k c", p=128), [], [f"slab{i}"])
        return v, f"slab{i}"

    def w_slab(w, r0, rows, c0, cols):
        return slab_load([(0, cols, w[r0:r0 + rows, c0:c0 + cols])], (rows // 128, cols))

    def rstd_from_ss(ss_ap, n, res):
        act(ss_ap, ss_ap, AF.Ln, [res, "st1"], [res], scale=1.0 / n, bias=epsb[:, 0:1])
        act(ss_ap, ss_ap, AF.Exp, [res], [res], scale=-0.5)

    epsb = alloc([1], F32) if False else st1[:, 15:16]
    dma("sp", CST, cst, [], ["CST"])
    P.op("dve", lambda e: e.memset(st1, EPS), [], ["st1"])
    P.op("dve", lambda e: e.memset(ones, 1.0), [], ["ones"])
    P.op("dve", lambda e: e.memset(S, 0.0), [], ["S"])
    P.op("dve", lambda e: e.memset(Sst, 0.0), [], ["Sst"])
    P.op("dve", lambda e: e.memset(Sstbf, 0.0), [], ["Sstbf"])
    P.op("dve", lambda e: e.memset(hist, 0.0), [], ["hist"])
    cp(identb, ident, ["CST"], ["identb"])
    dma("sp", rnw, ret_norm_w.partition_broadcast(128), [], ["rnw"])
    dma("sp", snw, ssd_norm_w.partition_broadcast(128), [], ["snw"])
    dma("sp", dtb, dt_bias.partition_broadcast(128), [], ["dtb"])
    dma("sp", abc, a_log.partition_broadcast(128), [], ["abc"])
    dma("sp", dsk, d_skip.partition_broadcast(128), [], ["dsk"])
    dma("sp", nw, norm_mix_w.rearrange("(c p) -> p c", p=128), [], ["nw"], slow=True)
    dma("sp", nmw, norm_mlp_w.rearrange("(c p) -> p c", p=128), [], ["nmw"], slow=True)
    dma("sp", cb, conv_b.rearrange("(c p) -> p c", p=128), [], ["cb"], slow=True)
    for jj in range(4):
        dma("sp", cw[:, jj, :], conv_w[jj, :].rearrange("(c p) -> p c", p=128), [], ["cw"], slow=True)
    act(abc, abc, AF.Exp, ["abc"], ["abc"])
    tsc(abc, abc, -1.0, None, ALU.mult, ALU.bypass, ["abc"], ["abc"]) if False else \
        P.op("dve", lambda e: e.tensor_scalar_mul(out=abc, in0=abc, scalar1=-1.0), ["abc"], ["abc"])
    dma("pool", Wdt, w_in[:, O_DT:O_DT + 16].rearrange("(k p) c -> p k c", p=128), [], ["Wdt"])

    xnT = regA
    mix = regB

    def norm_transpose(load_fn, src_res, wvec, wres, dstT, dst_res, tag):
        carve_base = off[0]
        junk = alloc([2048], BF16)
        xb = [alloc([2048], BF16) for _ in range(2)]
        ssq = alloc([8], F32)
        for c in range(8):
            xt, xres = load_fn(c)
            act(junk, xt, AF.Square, [xres], [tag + "junk", (tag + "ss", c)], accum_out=ssq[:, c:c + 1])
            rstd_from_ss(ssq[:, c:c + 1], 2048, (tag + "ss", c))
            xbc = xb[c % 2]
            tsc(xbc, xt, ssq[:, c:c + 1], None, ALU.mult, ALU.bypass, [xres, (tag + "ss", c)], [(tag + "xb", c % 2)]) if False else \
                P.op("dve", lambda e, xbc=xbc, xt=xt, c=c: e.tensor_scalar_mul(out=xbc, in0=xt, scalar1=ssq[:, c:c + 1]),
                     [xres, (tag + "ss", c)], [(tag + "xb", c % 2)])
            for half in range(2):
                pt, pres = ps()
                ptb = pt.bitcast(BF16)
                for k in range(8):
                    dc = half * 8 + k
                    tr(ptb[:, k * 128:(k + 1) * 128], xbc[:, dc * 128:(dc + 1) * 128], identb,
                       [(tag + "xb", c % 2), "identb"], [pres])
                tt(dstT[:, half * 8:(half + 1) * 8, c * 128:(c + 1) * 128],
                   ptb[:, 0:1024].rearrange("p (k t) -> p k t", k=8),
                   wvec[:, half * 8:(half + 1) * 8].unsqueeze(2).to_broadcast([128, 8, 128]),
                   ALU.mult, [pres, wres], [(dst_res, c)])
        off[0] = carve_base

    for pas in range(2):
        main = pas == 1
        xsrc = xm if main else xp
        carve()
        xbuf = [alloc([2048], F32) for _ in range(2)]

        def load_x(c, xsrc=xsrc, xbuf=xbuf):
            b = xbuf[c % 2]
            dma("sp", b, xsrc[c * 128:(c + 1) * 128, :], [], [("xbuf", c % 2)])
            return b, ("xbuf", c % 2)

        norm_transpose(load_x, None, nw, "nw", xnT, "xnT", "nx")
        P.barrier()

        def step(g):
            try:
                next(g)
                return True
            except StopIteration:
                return False

        def drive(tasks, prod, cons, ratio, pre=None):
            p0 = prod(*tasks[0], 0)
            if pre is not None:
                step(pre)
                for _ in range(3):
                    step(p0)
                while step(pre):
                    pass
            while step(p0):
                pass
            for i, tk in enumerate(tasks):
                cg = cons(*tk, i % 2)
                pg = prod(*tasks[i + 1], (i + 1) % 2) if i + 1 < len(tasks) else None
                pa = pg is not None
                ca = True
                cred = 0.0
                while ca:
                    ca = step(cg)
                    if pa:
                        cred += ratio
                        while pa and cred >= 1.0:
                            pa = step(pg)
                            cred -= 1.0
                while pa:
                    pa = step(pg)

        carve()
        cs = alloc([2, NT], F32)
        dma("sp", cs, rope[:, pas, :, :], [], ["cs"])
        kTr = [[alloc([512], BF16) for _ in range(2)] for _ in range(2)]
        qTr = [[alloc([512], BF16) for _ in range(2)] for _ in range(2)]
        ktok = [alloc([4, 256], BF16) for _ in range(2)]
        vtok = [alloc([4, 256], BF16) for _ in range(2)]
        gs = [alloc([4, 256], F32) for _ in range(2)]
        rt = [alloc([512], F32) for _ in range(2)]
        sT = alloc([128], BF16)
        tmpY = alloc([256], F32)
        yb = alloc([256], F32)
        yb2 = alloc([256], F32)
        junkr = alloc([256], BF16)
        assert off[0] <= c_off + CBYTES, off[0] - c_off

        def proj_rope(W, wres, t0, dst, dres):
            pk = []
            for a in range(2):
                p_, r_ = ps()
                for dc in range(16):
                    mm(p_[:, 0:512], W[:, dc, a * 128:(a + 1) * 128], xnT[:, dc, t0:t0 + 512], dc == 0, dc == 15,
                       [wres, "xnT"], [r_])
                pk.append((p_, r_))
            cosv = cs[:, 0, t0:t0 + 512]
            sinv = cs[:, 1, t0:t0 + 512]
            (p0, r0), (p1, r1) = pk
            tt(rt[0], p0[:, 0:512], cosv, ALU.mult, [r0, "cs"], ["rt0"])
            tt(rt[1], p1[:, 0:512], sinv, ALU.mult, [r1, "cs"], ["rt1"])
            tt(dst[0], rt[0], rt[1], ALU.subtract, ["rt0", "rt1"], [(dres, 0)])
            tt(rt[0], p0[:, 0:512], sinv, ALU.mult, [r0, "cs"], ["rt0"])
            tt(rt[1], p1[:, 0:512], cosv, ALU.mult, [r1, "cs"], ["rt1"])
            tt(dst[1], rt[0], rt[1], ALU.add, ["rt0", "rt1"], [(dres, 1)])

        def proj_tok_half(W, wres, t0, jj, func, dst, dres):
            p_, r_ = ps()
            for j2 in range(2):
                j = jj * 2 + j2
                for dc in range(16):
                    mm(p_[:, j2 * 256:(j2 + 1) * 256], xnT[:, dc, t0 + j * 128:t0 + (j + 1) * 128], W[:, dc, :],
                       dc == 0, dc == 15, [wres, "xnT"], [r_])
            act(dst[:, jj * 2:(jj + 1) * 2, :], p_[:, 0:512].rearrange("p (j c) -> p j c", j=2), func,
                [r_], [(dres, jj)])

        rslabs = {}

        def ret_producer(h, blk, par):
            if blk == 0:
                d_ = {"k": w_slab(w_in, 0, D, O_K + h * 256, 256), "v": w_slab(w_in, 0, D, O_V + h * 256, 256)}
                if main:
                    d_["q"] = w_slab(w_in, 0, D, O_Q + h * 256, 256)
                    d_["g"] = w_slab(w_in, 0, D, O_G + h * 256, 256)
                rslabs[h] = d_
            Wd = rslabs[h]
            t0 = blk * 512
            kT = kTr[par]
            proj_rope(Wd["k"][0], Wd["k"][1], t0, kT, f"kTr{par}")
            yield
            pt, pres = ps()
            ptb = pt.bitcast(BF16)
            for j in range(4):
                for a in range(2):
                    tr(ptb[:, j * 256 + a * 128:j * 256 + (a + 1) * 128], kT[a][:, j * 128:(j + 1) * 128], identb,
                       [(f"kTr{par}", a), "identb"], [pres])
            act(ktok[par], ptb[:, 0:1024].rearrange("p (j c) -> p j c", j=4), AF.Copy, [pres, "CST"], [f"ktok{par}"],
                scale=kd[:, h:h + 1])
            yield
            for jj in range(2):
                proj_tok_half(Wd["v"][0], Wd["v"][1], t0, jj, AF.Copy, vtok[par], f"vtok{par}")
                yield
            if main:
                proj_rope(Wd["q"][0], Wd["q"][1], t0, qTr[par], f"qTr{par}")
                yield
                for jj in range(2):
                    proj_tok_half(Wd["g"][0], Wd["g"][1], t0, jj, AF.Silu, gs[par], f"gs{par}")
                    yield

        def ret_consumer(h, blk, par):
            Sh = S[:, h, :]
            kT, qT, kt, vt, gsp = kTr[par], qTr[par], ktok[par], vtok[par], gs[par]
            if blk == 0:
                act(Sbf, Sh, AF.Copy, [("S", h)], ["Sbf"])
            for j in range(4):
                c = blk * 4 + j
                tsl = slice(j * 128, (j + 1) * 128)
                vres = (f"vtok{par}", j // 2)
                if main:
                    pS, rS = ps()
                    for a in range(2):
                        mm(pS[:, 0:128], kT[a][:, tsl], qT[a][:, tsl], a == 0, a == 1,
                           [(f"kTr{par}", a), (f"qTr{par}", a)], [rS])
                    tt(sT, pS[:, 0:128], dmask[:, h, :], ALU.mult, [rS, "CST"], ["sT"])
                    yield
                    pY, rY = ps()
                    mm(pY[:, 0:256], sT, vt[:, j, :], True, True, ["sT", vres], [rY])
                    for a in range(2):
                        mm(pY[:, 256:512], qT[a][:, tsl], Sbf[:, a * 256:(a + 1) * 256], a == 0, a == 1,
                           [(f"qTr{par}", a), "Sbf"], [rY])
                    act(tmpY, pY[:, 256:512], AF.Copy, [rY, "CST"], ["tmpY"], scale=qd[:, h:h + 1])
                    tt(yb, pY[:, 0:256], tmpY, ALU.add, [rY, "tmpY"], ["yb"])
                    act(junkr, yb, AF.Square, ["yb"], ["junkr", "ssr"], accum_out=st1[:, 0:1])
                    rstd_from_ss(st1[:, 0:1], 256, "ssr")
                    stt(yb2, yb, st1[:, 0:1], rnw[:, h * 256:(h + 1) * 256], ALU.mult, ALU.mult,
                        ["yb", "ssr", "rnw"], ["yb2"])
                    tt(mix[:, c, h * 256:(h + 1) * 256], yb2, gsp[:, j, :], ALU.mult, ["yb2", (f"gs{par}", j // 2)],
                       [("mix", c)])
                    yield
                if not (main and c == 7):
                    pK, rK = ps()
                    for a in range(2):
                        mm(pK[:, a * 256:(a + 1) * 256], kt[:, j, a * 128:(a + 1) * 128], vt[:, j, :], True, True,
                           [f"ktok{par}", vres], [rK])
                    stt(Sh, Sh, cdec[h], pK[:, 0:512], ALU.mult, ALU.add, [rK, ("S", h)], [("S", h)])
                    act(Sbf, Sh, AF.Copy, [("S", h)], ["Sbf"])
                    yield

        drive([(h, blk) for h in range(4) for blk in range(2)], ret_producer, ret_consumer,
              (7.0 / 12.0) if main else 1.0)
        P.barrier()

        carve()
        BL = 256
        U = alloc([6, BL + 3], F32)
        accb = alloc([BL], F32)
        xsT = alloc([4, BL], F32)
        BT = [alloc([BL], BF16) for _ in range(2)]
        CTt = [alloc([BL], BF16) for _ in range(2)]
        xstok = [alloc([2, 512], F32) for _ in range(2)]
        Btok = [alloc([2, 128], BF16) for _ in range(2)]
        zs = [alloc([2, 512], F32) for _ in range(2)]
        Xb = alloc([8, 128], F32)
        Lb = alloc([8, 128], F32)
        cbs = alloc([128], F32)
        Mb = alloc([8, 128], BF16)
        xc = alloc([512], BF16)
        xcd = alloc([512], BF16)
        t1 = alloc([512], F32)
        t2 = alloc([512], F32)
        yg = alloc([512], F32)
        junks = alloc([512], BF16)
        tsm8 = alloc([8, 16], F32)
        assert off[0] <= c_off + CBYTES, off[0] - c_off

        def bh(ap16, g):
            return ap16[:, g * 8:(g + 1) * 8].unsqueeze(2).to_broadcast([128, 8, 64])

        def v3(ap512):
            return ap512.rearrange("p (h q) -> p h q", h=8)

        def c8(ap128):
            return ap128.rearrange("p (c h) -> p c h", c=8)

        def dt_batch():
            pD, rD = ps()
            for c in range(8):
                for dc in range(16):
                    mm(pD[:, c * 16:(c + 1) * 16], xnT[:, dc, c * 128:(c + 1) * 128], Wdt[:, dc, :], dc == 0, dc == 15,
                       ["Wdt", "xnT"], [rD])
            tt(tsm8, c8(pD[:, 0:128]), dtb.unsqueeze(1).to_broadcast([128, 8, 16]), ALU.add, [rD, "dtb"], ["tsm8"])
            act(tsm8, tsm8, AF.Exp, ["tsm8"], ["tsm8"])
            act(SM[:, K_DT], tsm8, AF.Ln, ["tsm8", "ones"], ["SM"], bias=ones[:, 0:1])
            tt(SM[:, K_AC], SM[:, K_DT], abc.unsqueeze(1).to_broadcast([128, 8, 16]), ALU.mult, ["SM", "abc"], ["SM"])
            yield
            pA, rA = ps()
            acf = SM[:, K_AC].rearrange("p c h -> p (c h)")
            mm(pA[:, 0:128], tri, acf, True, True, ["CST", "SM"], [rA])
            mm(pA[:, 128:256], ones, acf, True, True, ["ones", "SM"], [rA])
            act(SM[:, K_ACS], c8(pA[:, 0:128]), AF.Copy, [rA], ["SM"])
            act(SM[:, K_AL], c8(pA[:, 128:256]), AF.Copy, [rA], ["SM"])
            tt(tsm8, SM[:, K_AL], SM[:, K_ACS], ALU.subtract, ["SM"], ["tsm8"])
            act(tsm8, tsm8, AF.Exp, ["tsm8"], ["tsm8"])
            tt(SM[:, K_DTDS], tsm8, SM[:, K_DT], ALU.mult, ["tsm8", "SM"], ["SM"])
            act(SM[:, K_EA], SM[:, K_ACS], AF.Exp, ["SM"], ["SM"])
            act(SM[:, K_CD], SM[:, K_AL], AF.Exp, ["SM"], ["SM"])
            act(SM[:, K_NACS], SM[:, K_ACS], AF.Copy, ["SM"], ["SM"], scale=-1.0)
            yield

        sslabs = {}

        def ssd_producer(g, blk, par):
            if blk == 0:
                d_ = {"x0": w_slab(w_in, 0, D, O_X + g * 512, 256), "x1": w_slab(w_in, 0, D, O_X + g * 512 + 256, 256),
                      "bc": slab_load([(0, 128, w_in[:, O_B + g * 128:O_B + (g + 1) * 128]),
                                       (128, 256, w_in[:, O_C + g * 128:O_C + (g + 1) * 128])], (16, 256))}
                if main:
                    d_["z0"] = w_slab(w_in, 0, D, O_Z + g * 512, 256)
                    d_["z1"] = w_slab(w_in, 0, D, O_Z + g * 512 + 256, 256)
                sslabs[g] = d_
            Wd = sslabs[g]
            t0 = blk * BL
            for t in range(6):
                W, wr = Wd[("x0", "x0", "x1", "x1", "bc", "bc")[t]]
                co = (t % 2) * 128
                ch = (g * 4 + t) if t < 4 else (8 + g if t == 4 else 10 + g)
                pU, rU = ps()
                for dc in range(16):
                    mm(pU[:, 0:BL], W[:, dc, co:co + 128], xnT[:, dc, t0:t0 + BL], dc == 0, dc == 15, [wr, "xnT"], [rU])
                cp(U[:, t, 0:3], hist[:, ch, :], [("hist", ch)], [("U", t)])
                act(U[:, t, 3:BL + 3], pU[:, 0:BL], AF.Copy, [rU], [("U", t)])
                cp(hist[:, ch, :], U[:, t, BL:BL + 3], [("U", t)], [("hist", ch)])
                tsc(accb, U[:, t, 0:BL], cw[:, 0, ch:ch + 1], cb[:, ch:ch + 1], ALU.mult, ALU.add,
                    [("U", t), "cw", "cb"], ["acc"])
                for jj in range(1, 4):
                    stt(accb, U[:, t, jj:jj + BL], cw[:, jj, ch:ch + 1], accb, ALU.mult, ALU.add,
                        [("U", t), "cw", "acc"], ["acc"])
                if t < 4:
                    act(xsT[:, t, :], accb, AF.Silu, ["acc"], [("xsT", t)])
                elif t == 4:
                    act(BT[par], accb, AF.Silu, ["acc"], [f"BT{par}"])
                else:
                    act(CTt[par], accb, AF.Silu, ["acc"], [f"CT{par}"])
                yield
            for j in range(2):
                pX, rX = ps()
                for t in range(4):
                    tr(pX[:, t * 128:(t + 1) * 128], xsT[:, t, j * 128:(j + 1) * 128], ident, [("xsT", t), "CST"], [rX])
                act(xstok[par][:, j, :], pX[:, 0:512], AF.Copy, [rX], [(f"xstok{par}", j)])
            yield
            pB, rB = ps()
            pBb = pB.bitcast(BF16)
            for j in range(2):
                tr(pBb[:, j * 128:(j + 1) * 128], BT[par][:, j * 128:(j + 1) * 128], identb, [f"BT{par}", "identb"], [rB])
            cp(Btok[par], pBb[:, 0:256].rearrange("p (j c) -> p j c", j=2), [rB], [f"Btok{par}"])
            yield
            if main:
                for j in range(2):
                    pZ, rZ = ps()
                    for q, zk in enumerate(("z0", "z1")):
                        Wz, rz = Wd[zk]
                        for dc in range(16):
                            mm(pZ[:, q * 256:(q + 1) * 256], xnT[:, dc, t0 + j * 128:t0 + (j + 1) * 128], Wz[:, dc, :],
                               dc == 0, dc == 15, [rz, "xnT"], [rZ])
                    act(zs[par][:, j, :], pZ[:, 0:512], AF.Silu, [rZ], [(f"zs{par}", j)])
                    yield

        def ssd_consumer(g, blk, par):
            for j in range(2):
                c = blk * 2 + j
                tsl = slice(j * 128, (j + 1) * 128)
                xsj = v3(xstok[par][:, j, :])
                xres = (f"xstok{par}", j)
                if main:
                    tt(Xb, SM[:, K_AC, c, g * 8:(g + 1) * 8].unsqueeze(2).to_broadcast([128, 8, 128]),
                       tri.unsqueeze(1).to_broadcast([128, 8, 128]), ALU.mult, ["SM", "CST"], ["Xb"])
                    pR = []
                    for q in range(2):
                        p_, r_ = ps()
                        mm(p_[:, 0:512], ones, Xb[:, q * 4:(q + 1) * 4, :].rearrange("p h l -> p (h l)"), True, False,
                           ["ones", "Xb"], [r_])
                        mm(p_[:, 0:512], ident, negm, False, True, ["CST"], [r_])
                        pR.append((p_, r_))
                    for hh in range(8):
                        p_, r_ = pR[hh // 4]
                        act(Lb[:, hh, :], p_[:, (hh % 4) * 128:(hh % 4 + 1) * 128], AF.Exp, [r_, "SM"], ["Lb"],
                            bias=SM[:, K_NACS, c, g * 8 + hh:g * 8 + hh + 1])
                    pC, rC = ps()
                    mm(pC[:, 0:128], BT[par][:, tsl], CTt[par][:, tsl], True, True, [f"BT{par}", f"CT{par}"], [rC])
                    act(cbs, pC[:, 0:128], AF.Copy, [rC], ["cbs"])
                    yield
                    tt(Mb, Lb, cbs.unsqueeze(1).to_broadcast([128, 8, 128]), ALU.mult, ["Lb", "cbs"], ["Mb"])
                    tt(v3(xc), xsj, bh(SM[:, K_DT, c, :], g), ALU.mult, [xres, "SM"], ["xc"])
                    pYd, rYd = ps()
                    for hh in range(8):
                        mm(pYd[:, hh * 64:(hh + 1) * 64], Mb[:, hh, :], xc[:, hh * 64:(hh + 1) * 64], True, True,
                           ["Mb", "xc"], [rYd])
                    pYo, rYo = ps()
                    mm(pYo[:, 0:512], CTt[par][:, tsl], Sstbf[:, g, :], True, True, [f"CT{par}", ("Sstbf", g)], [rYo])
                    tt(v3(t1), v3(pYo[:, 0:512]), bh(SM[:, K_EA, c, :], g), ALU.mult, [rYo, "SM"], ["t1"])
                    tt(t1, t1, pYd[:, 0:512], ALU.add, ["t1", rYd], ["t1"])
                    tt(v3(t2), xsj, bh(dsk, g), ALU.mult, [xres, "dsk"], ["t2"])
                    tt(t1, t1, t2, ALU.add, ["t1", "t2"], ["t1"])
                    tt(yg, t1, zs[par][:, j, :], ALU.mult, ["t1", (f"zs{par}", j)], ["yg"])
                    act(junks, yg, AF.Square, ["yg"], ["junks", "sss"], accum_out=st1[:, 1:2])
                    rstd_from_ss(st1[:, 1:2], 512, "sss")
                    stt(mix[:, c, 1024 + g * 512:1024 + (g + 1) * 512], yg, st1[:, 1:2], snw[:, g * 512:(g + 1) * 512],
                        ALU.mult, ALU.mult, ["yg", "sss", "snw"], [("mix", c)])
                    yield
                if not (main and c == 7):
                    tt(v3(xcd), xsj, bh(SM[:, K_DTDS, c, :], g), ALU.mult, [xres, "SM"], ["xcd"])
                    pSt, rSt = ps()
                    mm(pSt[:, 0:512], Btok[par][:, j, :], xcd, True, True, [f"Btok{par}", "xcd"], [rSt])
                    Sg = Sst[:, g, :]
                    tt(v3(Sg), v3(Sg), bh(SM[:, K_CD, c, :], g), ALU.mult, [("Sst", g), "SM"], [("Sst", g)])
                    tt(Sg, Sg, pSt[:, 0:512], ALU.add, [("Sst", g), rSt], [("Sst", g)])
                    if (not main) and c == 7:
                        P.op("dve", lambda e, Sg=Sg: e.tensor_scalar_mul(out=Sg, in0=Sg, scalar1=flag),
                             [("Sst", g), "CST"], [("Sst", g)])
                    act(Sstbf[:, g, :], Sg, AF.Copy, [("Sst", g)], [("Sstbf", g)])
                    yield

        drive([(g, blk) for g in range(2) for blk in range(4)], ssd_producer, ssd_consumer,
              (10.0 / 6.0) if main else 4.0, pre=dt_batch())
        P.barrier()

    carve()
    h1 = alloc([8, 2048], F32)
    mixT = regA
    if dbg:
        for c in range(8):
            dma("sp", dbg_mix[c * 128:(c + 1) * 128, :], mix[:, c, :], [("mix", c)], ["dbgmix"])
    for c in range(8):
        dma("sp", h1[:, c, :], xm[c * 128:(c + 1) * 128, :], [], [("h1", c)])
        for half in range(2):
            pt, pres = ps()
            ptb = pt.bitcast(BF16)
            for k in range(8):
                fc = half * 8 + k
                tr(ptb[:, k * 128:(k + 1) * 128], mix[:, c, fc * 128:(fc + 1) * 128], identb, [("mix", c), "identb"], [pres])
            eng = "act" if half == 0 else "dve"
            cp(mixT[:, half * 8:(half + 1) * 8, c * 128:(c + 1) * 128], ptb[:, 0:1024].rearrange("p (k t) -> p k t", k=8),
               [pres], [("mixT", c)], eng=eng) if eng == "dve" else \
                act(mixT[:, half * 8:(half + 1) * 8, c * 128:(c + 1) * 128], ptb[:, 0:1024].rearrange("p (k t) -> p k t", k=8),
                    AF.Copy, [pres], [("mixT", c)])
    for s in range(8):
        Wo, wor = w_slab(w_out, 0, D, s * 256, 256)
        for c2 in range(4):
            p_, r_ = ps()
            for cc in range(2):
                c = c2 * 2 + cc
                for fc in range(16):
                    mm(p_[:, cc * 256:(cc + 1) * 256], mixT[:, fc, c * 128:(c + 1) * 128], Wo[:, fc, :], fc == 0, fc == 15,
                       [wor, ("mixT", c)], [r_])
            for cc in range(2):
                c = c2 * 2 + cc
                tt(h1[:, c, s * 256:(s + 1) * 256], h1[:, c, s * 256:(s + 1) * 256], p_[:, cc * 256:(cc + 1) * 256], ALU.add,
                   [r_, ("h1", c)], [("h1", c)])
    P.barrier()
    if dbg:
        for c in range(8):
            dma("sp", dbg_h1[c * 128:(c + 1) * 128, :], h1[:, c, :], [("h1", c)], ["dbgh1"])

    hnT = regB.rearrange("p a b -> p (a b)").rearrange("p (a b) -> p a b", a=16)
    uT = regA

    def load_h(c):
        return h1[:, c, :], ("h1", c)

    off[0] = s_off
    norm_transpose(load_h, None, nmw, "nmw", hnT, "hnT", "nh")
    rl = [alloc([512], F32) for _ in range(2)]
    assert off[0] <= s_off + 27000
    for fb in range(4):
        for s in range(8):
            Wu, wur = w_slab(w_up, 0, D, fb * 2048 + s * 256, 256)
            for f2 in range(2):
                fc = s * 2 + f2
                pp = [ps(), ps()]
                for dc in range(16):
                    for half in range(2):
                        mm(pp[half][0][:, 0:512], Wu[:, dc, f2 * 128:(f2 + 1) * 128], hnT[:, dc, half * 512:(half + 1) * 512],
                           dc == 0, dc == 15, [wur, "hnT"], [pp[half][1]])
                for half in range(2):
                    act(rl[half], pp[half][0][:, 0:512], AF.Relu, [pp[half][1]], [("rl", half)])
                    tt(uT[:, fc, half * 512:(half + 1) * 512], rl[half], rl[half], ALU.mult, [("rl", half)], [("uT", fc)])
        for db in range(4):
            WA, war = w_slab(w_down, fb * 2048, 1024, db * 512, 512)
            WB, wbr = w_slab(w_down, fb * 2048 + 1024, 1024, db * 512, 512)
            for c in range(8):
                p_, r_ = ps()
                for fc in range(16):
                    Wd, wdr = (WA, war) if fc < 8 else (WB, wbr)
                    mm(p_[:, 0:512], uT[:, fc, c * 128:(c + 1) * 128], Wd[:, fc % 8, :], fc == 0, fc == 15,
                       [wdr, ("uT", fc)], [r_])
                tt(h1[:, c, db * 512:(db + 1) * 512], h1[:, c, db * 512:(db + 1) * 512], p_[:, 0:512], ALU.add,
                   [r_, ("h1", c)], [("h1", c)])
    P.barrier()

    nfw = regA[:, 0:4, :].rearrange("p a b -> p (a b)").bitcast(F32)
    ob = [regB[:, 0:2, :].rearrange("p a b -> p (a b)").bitcast(F32),
          regB[:, 2:4, :].rearrange("p a b -> p (a b)").bitcast(F32)]
    junkf = regB[:, 4, :]
    dma("sp", nfw, norm_final_w.partition_broadcast(128), [], ["nfw"])
    for c in range(8):
        act(junkf, h1[:, c, :], AF.Square, [("h1", c)], ["junkf", ("ssf", c)], accum_out=st1[:, 2 + c:3 + c])
        rstd_from_ss(st1[:, 2 + c:3 + c], 2048, ("ssf", c))
        o = ob[c % 2]
        stt(o, h1[:, c, :], st1[:, 2 + c:3 + c], nfw, ALU.mult, ALU.mult, [("h1", c), ("ssf", c), "nfw"], [("ob", c % 2)])
        dma("sp", yout[c * 128:(c + 1) * 128, :], o, [("ob", c % 2)], ["yout"])
    P.op("sp", None, ["yout", "dbgmix", "dbgh1"], [])
    _LAST[0] = P
    P.emit()
    es.close()
    return nc


def _consts(half):
    j = np.arange(128, dtype=np.float64)
    inv_freq = (10000.0 ** (-(np.arange(128, dtype=np.float32)) / np.float32(128))).astype(np.float32).astype(np.float64)
    rope = np.zeros((128, 2, 2, NT), np.float32)
    for pas in range(2):
        if half == 1:
            pos = np.arange(NT, dtype=np.float64) + pas * NT
        else:
            pos = np.arange(NT, dtype=np.float64)
        ang = (pos[None, :].astype(np.float32) * inv_freq[:, None].astype(np.float32)).astype(np.float32).astype(np.float64)
        rope[:, pas, 0, :] = np.cos(ang)
        rope[:, pas, 1, :] = np.sin(ang)
    cst = np.zeros((128, C_END), np.float32)
    idx = np.arange(128, dtype=np.float64)
    for h in range(4):
        lg = np.log(1.0 - 2.0 ** (-5.0 - h))
        rel = idx[None, :] - idx[:, None]
        dm = np.where(rel >= 0, np.exp(np.maximum(rel, 0) * lg), 0.0) * (256.0 ** -0.5)
        cst[:, C_DMASK + h * 128:C_DMASK + (h + 1) * 128] = dm
        cst[:, C_KD + h] = np.exp((127.0 - idx) * lg) * (256.0 ** -0.5)
        cst[:, C_QD + h] = np.exp((idx + 1.0) * lg)
    cst[:, C_TRI:C_TRI + 128] = (idx[:, None] <= idx[None, :]).astype(np.float32)
    neg = np.where(idx[None, :] >= idx[:, None], 0.0, -30000.0)
    cst[:, C_NEG:C_NEG + 512] = np.tile(neg, (1, 4))
    cst[:, C_ID:C_ID + 128] = np.eye(128)
    cst[:, C_FLAG] = float(half)
    return rope, cst


_NC_CACHE = {}
_LAST = [None]


def kernel(x, norm_mix_w, w_in, ret_norm_w, conv_w, conv_b, dt_bias, a_log, d_skip, ssd_norm_w, w_out,
           norm_mlp_w, w_up, w_down, norm_final_w, _dbg=False):
    x = np.ascontiguousarray(np.asarray(x, dtype=np.float32))
    if _dbg not in _NC_CACHE:
        _NC_CACHE[_dbg] = build_program(dbg=_dbg)
    nc = _NC_CACHE[_dbg]
    shared = {
        "w_in": np.ascontiguousarray(w_in, dtype=np.float32), "w_out": np.ascontiguousarray(w_out, dtype=np.float32),
        "w_up": np.ascontiguousarray(w_up, dtype=np.float32), "w_down": np.ascontiguousarray(w_down, dtype=np.float32),
        "norm_mix_w": np.asarray(norm_mix_w, np.float32), "ret_norm_w": np.asarray(ret_norm_w, np.float32),
        "conv_w": np.asarray(conv_w, np.float32), "conv_b": np.asarray(conv_b, np.float32),
        "dt_bias": np.asarray(dt_bias, np.float32), "a_log": np.asarray(a_log, np.float32),
        "d_skip": np.asarray(d_skip, np.float32), "ssd_norm_w": np.asarray(ssd_norm_w, np.float32),
        "norm_mlp_w": np.asarray(norm_mlp_w, np.float32), "norm_final_w": np.asarray(norm_final_w, np.float32),
    }
    in_maps = []
    for core in range(8):
        b, half = core // 2, core % 2
        rope, cst = _consts(half)
        m = dict(shared)
        m["xm"] = np.ascontiguousarray(x[b, half * NT:(half + 1) * NT, :])
        m["xp"] = np.ascontiguousarray(x[b, 0:NT, :]) if half == 1 else np.zeros((NT, D), np.float32)
        m["rope"] = rope
        m["cst"] = cst
        in_maps.append(m)
    res = run_bass_kernel_spmd(nc, in_maps, core_ids=list(range(8)))
    out = np.zeros((4, 2048, D), np.float32)
    for core in range(8):
        b, half = core // 2, core % 2
        out[b, half * NT:(half + 1) * NT, :] = res.results[core]["y"]
    if _dbg:
        return out, res
    return out
```
